# Optimizing a Trainium2 kernel written in Bass

```python
import jax, jax.numpy as jnp
from jax import lax
import numpy as np

D_MODEL = 2048
BATCH = 16
SEQ = 2048
DEPTH = 1

MEM_LEN = 256
NORM_EPS = 1e-6

LRU_WIDTH = D_MODEL
LRU_HEADS = 16
LRU_BLOCK = LRU_WIDTH // LRU_HEADS
CONV_WIDTH = 4
LRU_C = 8.0

SB_HEADS = 16
SB_HEAD_DIM = 64
SB_WIDTH = SB_HEADS * SB_HEAD_DIM
Q_BLOCK = 128

MEM_HEADS = 4
MEM_HEAD_DIM = 256
MEM_WIDTH = MEM_HEADS * MEM_HEAD_DIM

N_BRANCH = 3
D_FF = -(-8 * D_MODEL // (3 * 256)) * 256

IN_COLS = 2 * LRU_WIDTH + 3 * SB_WIDTH + MEM_WIDTH + N_BRANCH * D_MODEL
IN_SPLITS = [LRU_WIDTH,
             2 * LRU_WIDTH,
             2 * LRU_WIDTH + SB_WIDTH,
             2 * LRU_WIDTH + 2 * SB_WIDTH,
             2 * LRU_WIDTH + 3 * SB_WIDTH,
             2 * LRU_WIDTH + 3 * SB_WIDTH + MEM_WIDTH]

kernel_name = 'hybrid_rglru_stickbreak_memxattn_swiglu'


def rms_norm(x, g):
    xf = x.astype(jnp.float32)
    y = xf * lax.rsqrt(jnp.mean(xf * xf, axis=-1, keepdims=True) + NORM_EPS)
    return (y * g.astype(jnp.float32)).astype(x.dtype)


def rg_lru_block(u, gate_in, conv_w, conv_b, wa, ba, wx, bx, lam):
    B, T, W = u.shape
    c = lax.conv_general_dilated(u, conv_w[:, None, :].astype(u.dtype), window_strides=(1,),
                                 padding=[(CONV_WIDTH - 1, 0)],
                                 dimension_numbers=('NWC', 'WIO', 'NWC'),
                                 feature_group_count=W) + conv_b
    cb = c.reshape(B, T, LRU_HEADS, LRU_BLOCK)
    r = jax.nn.sigmoid(jnp.einsum('bthi,hij->bthj', cb, wa) + ba).reshape(B, T, W)
    i = jax.nn.sigmoid(jnp.einsum('bthi,hij->bthj', cb, wx) + bx).reshape(B, T, W)
    log_a = -LRU_C * r.astype(jnp.float32) * jax.nn.softplus(-lam.astype(jnp.float32))
    a = jnp.exp(log_a)
    mult = jnp.sqrt(jnp.maximum(-jnp.expm1(2.0 * log_a), 0.0))
    is_start = (jnp.arange(T) == 0)[None, :, None]
    mult = jnp.where(is_start, 1.0, mult)
    b_in = mult * (i * c).astype(jnp.float32)

    def step(h, inp):
        a_t, b_t = inp
        h = a_t * h + b_t
        return h, h

    h0 = jnp.zeros((B, W), jnp.float32)
    _, hs = lax.scan(step, h0, (jnp.swapaxes(a, 0, 1), jnp.swapaxes(b_in, 0, 1)))
    y = jnp.swapaxes(hs, 0, 1).astype(u.dtype)
    return y * jax.nn.gelu(gate_in)


def stick_breaking_attention(q, k, v):
    B, T, H, Dh = q.shape
    scale = Dh ** -0.5
    outs = []
    for blk in range(T // Q_BLOCK):
        q0 = blk * Q_BLOCK
        kv_len = q0 + Q_BLOCK
        qb = q[:, q0:kv_len]
        kb = k[:, :kv_len]
        vb = v[:, :kv_len]
        z = jnp.einsum('bqhd,bkhd->bhqk', qb, kb).astype(jnp.float32) * scale
        t_idx = q0 + jnp.arange(Q_BLOCK)[:, None]
        s_idx = jnp.arange(kv_len)[None, :]
        causal = s_idx < t_idx
        log_beta = jax.nn.log_sigmoid(z)
        log_1m_beta = jnp.where(causal, jax.nn.log_sigmoid(-z), 0.0)
        tail = lax.cumsum(log_1m_beta, axis=3, reverse=True) - log_1m_beta
        w = jnp.where(causal, jnp.exp(log_beta + tail), 0.0)
        outs.append(jnp.einsum('bhqk,bkhd->bqhd', w.astype(vb.dtype), vb))
    return jnp.concatenate(outs, axis=1)


def memory_cross_attention(q, k, v):
    s = jnp.einsum('bthd,bmhd->bhtm', q, k).astype(jnp.float32) * (q.shape[-1] ** -0.5)
    p = jax.nn.softmax(s, axis=-1).astype(v.dtype)
    return jnp.einsum('bhtm,bmhd->bthd', p, v)


def _normal(key, shape, fan_in):
    return jax.random.normal(key, shape, jnp.float32) * (fan_in ** -0.5)


def setup_inputs(seed: int = 0) -> dict:
    key = jax.random.key(seed)
    ks = jax.random.split(key, 26)
    L = DEPTH

    def gain(k, shape):
        return 1.0 + 0.02 * jax.random.normal(k, shape, jnp.float32)

    def bias(k, shape):
        return 0.01 * jax.random.normal(k, shape, jnp.float32)

    u = jax.random.uniform(ks[10], (L, LRU_WIDTH), jnp.float32, 0.9, 0.999)
    a0 = u ** (1.0 / LRU_C)
    lru_lambda = jnp.log(a0) - jnp.log1p(-a0)
    return {
        'x': jax.random.normal(ks[0], (BATCH, SEQ, D_MODEL), jnp.float32),
        'mem': jax.random.normal(ks[1], (BATCH, MEM_LEN, D_MODEL), jnp.float32),
        'g_mix': gain(ks[2], (L, D_MODEL)),
        'g_mem': gain(ks[3], (L, D_MODEL)),
        'w_in': _normal(ks[4], (L, D_MODEL, IN_COLS), D_MODEL),
        'b_gate': bias(ks[5], (L, N_BRANCH * D_MODEL)),
        'conv_w': _normal(ks[6], (L, CONV_WIDTH, LRU_WIDTH), CONV_WIDTH),
        'conv_b': bias(ks[7], (L, LRU_WIDTH)),
        'lru_wa': _normal(ks[8], (L, LRU_HEADS, LRU_BLOCK, LRU_BLOCK), LRU_BLOCK),
        'lru_ba': bias(ks[9], (L, LRU_HEADS, LRU_BLOCK)),
        'lru_wx': _normal(ks[11], (L, LRU_HEADS, LRU_BLOCK, LRU_BLOCK), LRU_BLOCK),
        'lru_bx': bias(ks[12], (L, LRU_HEADS, LRU_BLOCK)),
        'lru_lambda': lru_lambda,
        'sb_gq': gain(ks[13], (L, SB_HEAD_DIM)),
        'sb_gk': gain(ks[14], (L, SB_HEAD_DIM)),
        'mem_w_kv': _normal(ks[15], (L, D_MODEL, 2 * MEM_WIDTH), D_MODEL),
        'mem_gq': gain(ks[16], (L, MEM_HEAD_DIM)),
        'mem_gk': gain(ks[17], (L, MEM_HEAD_DIM)),
        'w_pa': _normal(ks[18], (L, LRU_WIDTH, D_MODEL), LRU_WIDTH),
        'w_pb': _normal(ks[19], (L, SB_WIDTH, D_MODEL), SB_WIDTH),
        'w_pc': _normal(ks[20], (L, MEM_WIDTH, D_MODEL), MEM_WIDTH),
        'w_out': _normal(ks[21], (L, D_MODEL, D_MODEL), D_MODEL),
        'g_ffn': gain(ks[22], (L, D_MODEL)),
        'w_fc': _normal(ks[23], (L, D_MODEL, 2 * D_FF), D_MODEL),
        'w_down': _normal(ks[24], (L, D_FF, D_MODEL), D_FF),
    }


def reference(x, mem, g_mix, g_mem, w_in, b_gate, conv_w, conv_b, lru_wa, lru_ba, lru_wx, lru_bx,
              lru_lambda, sb_gq, sb_gk, mem_w_kv, mem_gq, mem_gk, w_pa, w_pb, w_pc, w_out,
              g_ffn, w_fc, w_down):
    B, T, D = x.shape
    M = mem.shape[1]
    for l in range(DEPTH):
        h = rms_norm(x, g_mix[l])
        proj = h @ w_in[l]
        u_lru, u_gate, q_sb, k_sb, v_sb, q_mem, gate_logits = jnp.split(proj, IN_SPLITS, axis=-1)

        y_a = rg_lru_block(u_lru, u_gate, conv_w[l], conv_b[l], lru_wa[l], lru_ba[l],
                           lru_wx[l], lru_bx[l], lru_lambda[l])

        q_sb = rms_norm(q_sb.reshape(B, T, SB_HEADS, SB_HEAD_DIM), sb_gq[l])
        k_sb = rms_norm(k_sb.reshape(B, T, SB_HEADS, SB_HEAD_DIM), sb_gk[l])
        v_sb = v_sb.reshape(B, T, SB_HEADS, SB_HEAD_DIM)
        y_b = stick_breaking_attention(q_sb, k_sb, v_sb).reshape(B, T, SB_WIDTH)

        m = rms_norm(mem, g_mem[l])
        k_m, v_m = jnp.split(m @ mem_w_kv[l], 2, axis=-1)
        k_m = rms_norm(k_m.reshape(B, M, MEM_HEADS, MEM_HEAD_DIM), mem_gk[l])
        v_m = v_m.reshape(B, M, MEM_HEADS, MEM_HEAD_DIM)
        q_m = rms_norm(q_mem.reshape(B, T, MEM_HEADS, MEM_HEAD_DIM), mem_gq[l])
        y_c = memory_cross_attention(q_m, k_m, v_m).reshape(B, T, MEM_WIDTH)

        gates = jax.nn.sigmoid(gate_logits + b_gate[l]).reshape(B, T, N_BRANCH, D)
        merged = (gates[:, :, 0] * (y_a @ w_pa[l])
                  + gates[:, :, 1] * (y_b @ w_pb[l])
                  + gates[:, :, 2] * (y_c @ w_pc[l]))
        x = x + merged @ w_out[l]

        h2 = rms_norm(x, g_ffn[l])
        f_gate, f_up = jnp.split(h2 @ w_fc[l], 2, axis=-1)
        x = x + (jax.nn.silu(f_gate) * f_up) @ w_down[l]
    return x
```

```python
import numpy as np
import concourse.bass as bass
import concourse.mybir as mybir
from concourse.bass_utils import run_bass_kernel_spmd

F32 = mybir.dt.float32
BF16 = mybir.dt.bfloat16
U8 = mybir.dt.uint8
AF = mybir.ActivationFunctionType
ALU = mybir.AluOpType

D = 2048
KC = D // 128
TT = 512
NSUB = TT // 128
MEM = 256
DFF = 5632
NFF = DFF // 128
EPS = 1e-6
IN_COLS = 14336

CB_LRUX = 0
CB_LRUG = 4
CB_Q = 8
CB_K = 10
CB_V = 12
CB_QM = 14
CB_GATE = 16

CV_GMIX = 0
CV_GFFN = 16
CV_GMEM = 32
CV_CONVW = 48
CV_CONVB = 112
CV_BA = 128
CV_BX = 144
CV_LAM = 160
CV_BGATE = 176
CV_GQ = 224
CV_GK = 225
CV_MGQ = 226
CV_MGK = 228
CV_N = 230

C_IDENT = 0
C_MASK = 1
C_NUINCL = 2
C_NEGONES = 3
C_B64 = 4
C_O256 = 5
C_ONES = 6
NCONST = 7


class Buf:
    __slots__ = ("name", "w", "r")

    def __init__(self, name):
        self.name = name
        self.w = []
        self.r = []


class Eng:
    def __init__(self, nc, h, name, is_pe=False):
        self.h = h
        self.name = name
        self.sem = nc.alloc_semaphore("clk_" + name)
        self.cnt = 0
        self.seen = {}
        self.is_pe = is_pe
        self.nwaits = 0
        self.nops = 0


class Tracker:
    def __init__(self, nc):
        self.nc = nc
        self.pe = Eng(nc, nc.tensor, "pe", is_pe=True)
        self.act = Eng(nc, nc.scalar, "act")
        self.dve = Eng(nc, nc.vector, "dve")
        self.pool = Eng(nc, nc.gpsimd, "pool")
        self.sp = Eng(nc, nc.sync, "sp")

    def _collect(self, E, r, w):
        toks = []
        for b in r:
            for t in b.w:
                if t[0] is E.sem and E.is_pe:
                    continue
                toks.append(t)
        for b in w:
            for t in b.w:
                if t[0] is E.sem and E.is_pe:
                    continue
                toks.append(t)
            for t in b.r:
                if t[0] is E.sem and E.is_pe:
                    continue
                toks.append(t)
        return toks

    def _wait(self, E, toks):
        best = {}
        for sem, val in toks:
            k = id(sem)
            if E.seen.get(k, 0) >= val:
                continue
            if k not in best or best[k][1] < val:
                best[k] = (sem, val)
        for k, (sem, val) in best.items():
            E.h.wait_ge(sem, val)
            E.seen[k] = val
            E.nwaits += 1

    def _commit(self, tok, r, w):
        for b in r:
            b.r.append(tok)
            if len(b.r) > 64:
                newest = {}
                for t in b.r:
                    k = id(t[0])
                    if k not in newest or newest[k][1] < t[1]:
                        newest[k] = t
                b.r = list(newest.values())
        for b in w:
            b.w = [tok]
            b.r = []

    def op(self, E, fn, r=(), w=()):
        self._wait(E, self._collect(E, r, w))
        ins = fn()
        E.cnt += 1
        E.nops += 1
        ins.then_inc(E.sem, 1)
        self._commit((E.sem, E.cnt), r, w)
        return ins

    def group(self, E, fns, r=(), w=()):
        self._wait(E, self._collect(E, r, w))
        ins = None
        for fn in fns:
            ins = fn()
            E.nops += 1
        E.cnt += 1
        ins.then_inc(E.sem, 1)
        self._commit((E.sem, E.cnt), r, w)

    def dma(self, E, out, in_, semstate, r=(), w=()):
        self._wait(E, self._collect(E, r, w))
        E.h.dma_start(out=out, in_=in_).then_inc(semstate[0], 16)
        semstate[1] += 16
        self._commit((semstate[0], semstate[1]), r, w)

    def handoff(self, old, new):
        toks = []
        for b in old:
            toks += b.w + b.r
        newest = {}
        for t in toks:
            k = id(t[0])
            if k not in newest or newest[k][1] < t[1]:
                newest[k] = t
        toks = list(newest.values())
        for b in new:
            b.w = list(toks) + b.w
            b.r = list(b.r)


def build_nc(NSEQ=2, SEQ=2048, dbg=None):
    dbg = dbg or {}
    NT = SEQ // TT
    NTOK = NSEQ * SEQ
    nc = bass.Bass("TRN2", target_bir_lowering=False)
    T = Tracker(nc)
    PE, ACT, DVE, POOL, SP = T.pe, T.act, T.dve, T.pool, T.sp

    def dram_in(name, shape):
        return nc.dram_tensor(name, shape, F32, kind="ExternalInput").ap()

    x_d = dram_in("x", [NTOK, D])
    mem_d = dram_in("mem", [NSEQ * MEM, D])
    w_in_d = dram_in("w_in", [D, IN_COLS])
    w_kv_d = dram_in("w_kv", [D, 2048])
    w_pa_d = dram_in("w_pa", [2048, D])
    w_pb_d = dram_in("w_pb", [1024, D])
    w_pc_d = dram_in("w_pc", [1024, D])
    w_out_d = dram_in("w_out", [D, D])
    w_fc_d = dram_in("w_fc", [D, 2 * DFF])
    w_down_d = dram_in("w_down", [DFF, D])
    lru_w_d = dram_in("lru_w", [2, 16, 128, 128])
    cvec_d = dram_in("cvec", [128, CV_N])
    consts_d = dram_in("consts", [128, NCONST, 128])
    y_d = nc.dram_tensor("y", [NTOK, D], F32, kind="ExternalOutput").ap()

    wsrc = {"in": w_in_d, "kv": w_kv_d, "pa": w_pa_d, "pb": w_pb_d, "pc": w_pc_d,
            "out": w_out_d, "fc": w_fc_d, "down": w_down_d}
    wbf = {k: nc.dram_tensor("wb_" + k, list(v.shape), BF16, kind="Internal").ap() for k, v in wsrc.items()}
    wconv = {}
    NCS = 6
    csems = [[nc.alloc_semaphore(f"cv{i}"), 0] for i in range(NCS)]
    cstate = {"n": 0}

    def ensure_conv(mat, cb):
        key = (mat, cb)
        if key in wconv:
            return wconv[key]
        b = Buf(f"wb_{mat}_{cb}")
        i = cstate["n"]
        cstate["n"] += 1
        ss = csems[i % NCS]
        if ss[1] > 0:
            POOL.h.wait_ge(ss[0], ss[1])
        T.dma(POOL, wbf[mat][:, cb * 512:(cb + 1) * 512], wsrc[mat][:, cb * 512:(cb + 1) * 512], ss, w=[b])
        wconv[key] = b
        return b

    def sb(name, shape, dt):
        return nc.alloc_sbuf_tensor(name, shape, dt)

    kT = sb("kT", [128, 8, SEQ], BF16)
    vC = sb("vC", [128, SEQ // 128, 1024], BF16)
    kmT = sb("kmT", [128, 8, MEM], BF16)
    vm = sb("vm", [128, 2, 1024], BF16)
    hT = sb("hT", [128, KC, TT], BF16)
    R1 = sb("R1", [128, 32768], U8)
    MR2 = sb("MR2", [128, 16384 + 24576], U8)
    W = [sb("W0", [128, KC, 512], BF16), sb("W1", [128, KC, 512], BF16)]
    lruw = sb("lruw", [128, 2, 16, 128], BF16)
    cst = sb("cst", [128, NCONST, 128], BF16)
    cv = sb("cv", [128, CV_N], F32)
    dv = sb("dv", [128, 112], F32)
    hstate = sb("hstate", [128, 16], F32)
    halo = sb("halo", [128, 16, 3], F32)
    stat = sb("stat", [128, 16], F32)

    def view(region, off, shape, dt):
        nb = int(np.prod(shape[1:])) * (4 if dt == F32 else 2)
        v = region[:, off:off + nb].bitcast(dt)
        if len(shape) == 3:
            v = v.rearrange("p (a b) -> p a b", a=shape[1])
        return v

    yaT = view(R1, 0, [128, 16, TT], BF16)
    ybT = view(R1, 16384, [128, 8, TT], BF16)
    ycT = view(R1, 24576, [128, 8, TT], BF16)
    x1 = view(R1, 0, [128, NSUB, D], F32)
    mergedT = view(MR2, 0, [128, 16, TT], BF16)
    R2OFF = 16384

    def r2(off, shape, dt):
        return view(MR2, R2OFF + off, shape, dt)

    PS = nc.alloc_psum_tensor("ps", [128, 8, 512], F32)
    bankB = [Buf(f"bank{i}") for i in range(8)]

    def bank(i):
        return PS[:, i, :]

    B = {}

    def buf(name):
        if name not in B:
            B[name] = Buf(name)
        return B[name]

    def dsem(name):
        return [nc.alloc_semaphore(name), 0]

    wsem = [dsem("w0"), dsem("w1")]
    xsem = [dsem(f"x{i}") for i in range(4)]
    osem = [dsem(f"o{s}") for s in range(NSUB)]
    x1sem = [dsem(f"xr{s}") for s in range(NSUB)]
    misc_sem = dsem("misc")
    misc2_sem = dsem("misc2")
    misc3_sem = dsem("misc3")
    dbg_sem = dsem("dbg")

    Wbuf = [Buf("W0"), Buf("W1")]
    wrr = {"i": 0}

    def load_w(mat, cb, k0=0, nk=KC):
        cbuf = ensure_conv(mat, cb)
        s = wrr["i"] % 2
        wrr["i"] += 1
        src = wbf[mat][k0 * 128:(k0 + nk) * 128, cb * 512:(cb + 1) * 512].rearrange("(kc p) c -> p kc c", p=128)
        T.dma(SP, W[s][:, 0:nk, :], src, wsem[s], r=[cbuf], w=[Wbuf[s]])
        return W[s], Wbuf[s]

    cvB, cstB, lruwB, dvB = buf("cv"), buf("cst"), buf("lruw"), buf("dv")
    T.dma(SP, cv[:], cvec_d, misc_sem, w=[cvB])
    T.dma(POOL, cst[:], consts_d, misc2_sem, w=[cstB])
    T.dma(POOL, lruw[:].rearrange("p a h j -> p (a h) j"),
          lru_w_d.rearrange("a h i j -> i (a h) j"), misc3_sem, w=[lruwB])

    def cm(i):
        return cst[:, i, :]

    DV_GQ8, DV_MGQ, DV_NSP, DV_NSP2, DV_TMP, DV_NSPH, DV_HBA, DV_HBX = 0, 1, 16, 32, 48, 64, 80, 96
    T.op(DVE, lambda: nc.vector.tensor_scalar(out=dv[:, DV_GQ8:DV_GQ8 + 1], in0=cv[:, CV_GQ:CV_GQ + 1],
                                              scalar1=0.125, scalar2=None, op0=ALU.mult), r=[cvB], w=[dvB])
    T.op(DVE, lambda: nc.vector.tensor_scalar(out=dv[:, DV_MGQ:DV_MGQ + 2], in0=cv[:, CV_MGQ:CV_MGQ + 2],
                                              scalar1=1.0 / 16.0, scalar2=None, op0=ALU.mult), r=[cvB], w=[dvB])
    T.op(ACT, lambda: nc.scalar.activation(out=dv[:, DV_TMP:DV_TMP + 16], in_=cv[:, CV_LAM:CV_LAM + 16],
                                           func=AF.Exp, scale=-1.0), r=[cvB, dvB], w=[dvB])
    T.op(ACT, lambda: nc.scalar.activation(out=dv[:, DV_TMP:DV_TMP + 16], in_=dv[:, DV_TMP:DV_TMP + 16],
                                           func=AF.Ln, bias=1.0, scale=1.0), r=[dvB], w=[dvB])
    T.op(DVE, lambda: nc.vector.tensor_scalar(out=dv[:, DV_NSP:DV_NSP + 16], in0=dv[:, DV_TMP:DV_TMP + 16],
                                              scalar1=-8.0, scalar2=None, op0=ALU.mult), r=[dvB], w=[dvB])
    T.op(DVE, lambda: nc.vector.tensor_scalar(out=dv[:, DV_NSP2:DV_NSP2 + 16], in0=dv[:, DV_TMP:DV_TMP + 16],
                                              scalar1=-16.0, scalar2=None, op0=ALU.mult), r=[dvB], w=[dvB])
    T.op(DVE, lambda: nc.vector.tensor_scalar(out=dv[:, DV_NSPH:DV_NSPH + 16], in0=dv[:, DV_TMP:DV_TMP + 16],
                                              scalar1=-4.0, scalar2=None, op0=ALU.mult), r=[dvB], w=[dvB])
    T.op(DVE, lambda: nc.vector.tensor_scalar(out=dv[:, DV_HBA:DV_HBA + 16], in0=cv[:, CV_BA:CV_BA + 16],
                                              scalar1=0.5, scalar2=None, op0=ALU.mult), r=[cvB, dvB], w=[dvB])
    T.op(DVE, lambda: nc.vector.tensor_scalar(out=dv[:, DV_HBX:DV_HBX + 16], in0=cv[:, CV_BX:CV_BX + 16],
                                              scalar1=0.5, scalar2=None, op0=ALU.mult), r=[cvB, dvB], w=[dvB])

    dbg_outs = {}

    def dump(name, ap_sb, shape, dt, rbufs):
        if name not in dbg:
            return
        if name in dbg_outs:
            return
        t = nc.dram_tensor("dbg_" + name, shape, dt, kind="ExternalOutput").ap()
        dbg_outs[name] = t
        T.dma(SP, t, ap_sb, dbg_sem, r=rbufs, w=[buf("dbgdram_" + name)])

    psrr = {"i": 0}

    def next_bank(pool):
        i = pool[psrr.setdefault(id(pool), 0) % len(pool)]
        psrr[id(pool)] += 1
        return i

    def dense_ws(Wap, Wb, col0, actT, actB, nk, pb, ncols=TT, c0=0):
        fns = []
        for k in range(nk):
            fns.append(lambda k=k: nc.tensor.matmul(bank(pb)[:, 0:ncols], lhsT=Wap[:, k, col0:col0 + 128],
                                                    rhs=actT[:, k, c0:c0 + ncols], start=(k == 0), stop=(k == nk - 1)))
        T.group(PE, fns, r=[Wb] + actB, w=[bankB[pb]])

    def dense_as(Wap, Wb, actT, actB, nk, s, pb, kofs=0):
        fns = []
        for k in range(nk):
            fns.append(lambda k=k: nc.tensor.matmul(bank(pb), lhsT=actT[:, kofs + k, s * 128:(s + 1) * 128],
                                                    rhs=Wap[:, k, :], start=(k == 0), stop=(k == nk - 1)))
        T.group(PE, fns, r=[Wb] + actB, w=[bankB[pb]])

    PST = PS[:, 6:8, :].bitcast(BF16).rearrange("p a (b c) -> p (a b) c", c=128)

    PST2 = [PST, PS[:, 4:6, :].bitcast(BF16).rearrange("p a (b c) -> p (a b) c", c=128)]
    PSTB = [[bankB[6], bankB[7]], [bankB[4], bankB[5]]]

    def norm_transpose(src_rows, nrows_tiles, gcol, dstT, dstB, srcB_list, stage_off, load=True, srcs=None, nsub_cols=None):
        n = len(src_rows)
        if load:
            xs_v = [view(MR2, i * 8192, [128, D], F32) for i in range(4)]
            xn_v = [view(MR2, 32768 + i * 4096, [128, D], BF16) for i in range(2)]
        else:
            xs_v = []
            xn_v = [view(MR2, 24576 + i * 4096, [128, D], BF16) for i in range(4)]
        nxn = len(xn_v)
        xsB = [buf(f"xs{i}") for i in range(4)]
        xnB = [buf(f"xn{i}") for i in range(4)]
        stB = [buf(f"stat{i}") for i in range(4)]
        xin, xinB = [], []
        for s, src in enumerate(src_rows):
            if load:
                T.dma(SP, xs_v[s], src, xsem[s], w=[xsB[s]])
                xin.append(xs_v[s])
                xinB.append([xsB[s]])
            else:
                xin.append(src)
                xinB.append(srcs[s])
        for s in range(n):
            c = 2 * s
            T.op(DVE, lambda: nc.vector.memset(stat[:, c:c + 2], 0.0), w=[stB[s]])
            T.op(ACT, lambda: nc.scalar.activation(out=xn_v[s % nxn], in_=xin[s], func=AF.Square,
                                                   accum_out=stat[:, c:c + 1]), r=xinB[s] + [stB[s]], w=[xnB[s % nxn], stB[s]])
        for s in range(n):
            c = 2 * s
            T.op(ACT, lambda: nc.scalar.activation(out=stat[:, c + 1:c + 2], in_=stat[:, c:c + 1], func=AF.Sqrt,
                                                   bias=EPS, scale=1.0 / D), r=[stB[s]], w=[stB[s]])
            T.op(DVE, lambda: nc.vector.reciprocal(out=stat[:, c + 1:c + 2], in_=stat[:, c + 1:c + 2]), r=[stB[s]], w=[stB[s]])
        for s in range(n):
            c = 2 * s
            p = s % nxn
            T.op(ACT, lambda: nc.scalar.activation(out=xn_v[p], in_=xin[s], func=AF.Copy,
                                                   scale=stat[:, c + 1:c + 2]), r=xinB[s] + [stB[s]], w=[xnB[p]])
            pt = PST2[s % 2]
            fns = [(lambda k=k: nc.tensor.transpose(pt[:, k, :], xn_v[p][:, k * 128:(k + 1) * 128], cm(C_IDENT)))
                   for k in range(KC)]
            T.group(PE, fns, r=[xnB[p], cstB], w=PSTB[s % 2])
            T.op(DVE, lambda: nc.vector.tensor_tensor(
                out=dstT[:, :, s * 128:(s + 1) * 128], in0=pt,
                in1=cv[:, gcol:gcol + KC].unsqueeze(2).to_broadcast([128, KC, 128]), op=ALU.mult),
                r=PSTB[s % 2] + [cvB], w=[dstB])

    hTB = buf("hT")
    kTB, vCB = buf("kT"), buf("vC")
    kmTB, vmB = buf("kmT"), buf("vm")
    ybB, ycB = buf("ybT"), buf("ycT")
    yaBc = [buf(f"ya{h}") for h in range(16)]
    x1B = [buf(f"x1_{s}") for s in range(NSUB)]
    qTB = buf("qT")
    hstB, haloB = buf("hstate"), buf("halo")
    R2B = buf("R2scratch")

    r2_live = []

    def r2buf(name):
        b = buf(name)
        if b not in r2_live:
            r2_live.append(b)
        return b

    mgB = r2buf("mergedT")

    def r2_handoff(newnames):
        news = [r2buf(n) for n in newnames]
        olds = [b for b in r2_live if b not in news]
        T.handoff(olds, news)

    def mem_prep(b):
        r2_handoff(["xs0", "xs1", "xs2", "xs3", "xn0", "xn1", "xn2", "xn3"])
        rows = [mem_d[b * MEM + s * 128: b * MEM + (s + 1) * 128, :] for s in range(2)]
        norm_transpose(rows, 2, CV_GMEM, hT, hTB, None, 0)
        mT = hT
        sq_v = [r2(24576 - 4096 + i * 512, [128, MEM], BF16) for i in range(2)]
        sm_v = r2(24576 - 4096 + 1024, [128, MEM], F32)
        sqB = [r2buf("msq0"), r2buf("msq1")]
        smB = r2buf("msm")
        T.handoff([buf("xn1")], sqB + [smB])
        for cbk in range(2):
            Wap, Wb = load_w("kv", cbk)
            for hp in range(2):
                pbs = [0 + 2 * hp, 1 + 2 * hp]
                for e in range(2):
                    dense_ws(Wap, Wb, (2 * hp + e) * 128, mT, [hTB], KC, pbs[e], ncols=MEM)
                    T.op(ACT, lambda e=e: nc.scalar.activation(out=sq_v[e], in_=bank(pbs[e])[:, 0:MEM], func=AF.Square),
                         r=[bankB[pbs[e]]], w=[sqB[e]])
                T.group(PE, [lambda: nc.tensor.matmul(bank(4)[:, 0:MEM], lhsT=cm(C_O256), rhs=sq_v[0], start=True, stop=False),
                             lambda: nc.tensor.matmul(bank(4)[:, 0:MEM], lhsT=cm(C_O256), rhs=sq_v[1], start=False, stop=True)],
                        r=sqB + [cstB], w=[bankB[4]])
                T.op(ACT, lambda: nc.scalar.activation(out=sm_v, in_=bank(4)[:, 0:MEM], func=AF.Sqrt, bias=EPS, scale=1.0),
                     r=[bankB[4]], w=[smB])
                T.op(DVE, lambda: nc.vector.reciprocal(out=sm_v, in_=sm_v), r=[smB], w=[smB])
                for e in range(2):
                    ch = cbk * 4 + hp * 2 + e
                    T.op(DVE, lambda e=e, ch=ch: nc.vector.scalar_tensor_tensor(
                        out=kmT[:, ch, :], in0=bank(pbs[e])[:, 0:MEM], scalar=cv[:, CV_MGK + e:CV_MGK + e + 1],
                        in1=sm_v, op0=ALU.mult, op1=ALU.mult), r=[bankB[pbs[e]], smB, cvB], w=[kmTB])
        for cbv in range(2):
            Wap, Wb = load_w("kv", 2 + cbv)
            for mb in range(2):
                pb = 5 if mb == 0 else 3
                dense_as(Wap, Wb, mT, [hTB], KC, mb, pb)
                T.op(ACT, lambda mb=mb, pb=pb: nc.scalar.activation(out=vm[:, mb, cbv * 512:(cbv + 1) * 512], in_=bank(pb),
                                                                    func=AF.Copy), r=[bankB[pb]], w=[vmB])
        dump("kmT", kmT[:], [128, 8, MEM], BF16, [kmTB])
        dump("vm", vm[:], [128, 2, 1024], BF16, [vmB])

    def tile(b, qt):
        tok0 = b * SEQ + qt * TT
        first = (qt == 0)
        last_dbg = (b == NSEQ - 1 and qt == NT - 1)

        r2_handoff(["xs0", "xs1", "xs2", "xs3", "xn0", "xn1", "xn2", "xn3"])
        rows = [x_d[tok0 + s * 128: tok0 + (s + 1) * 128, :] for s in range(NSUB)]
        norm_transpose(rows, NSUB, CV_GMIX, hT, hTB, None, 0)
        if last_dbg:
            dump("hT", hT[:], [128, KC, TT], BF16, [hTB])

        def mr(off, shape, dt):
            return view(MR2, off, shape, dt)

        L_UB = 8192
        L_C = L_UB + 2 * 2080
        L_CBF = L_C + 3 * 2048
        L_R = L_CBF + 2048
        L_A = L_R + 2048
        L_IG = L_A + 2048
        assert L_IG + 2048 <= 40960
        ub_v = [mr(L_UB + i * 2080, [128, 520], F32) for i in range(2)]
        c_v = [mr(L_C + i * 2048, [128, TT], F32) for i in range(3)]
        cbf_v = [mr(L_CBF + i * 1024, [128, TT], BF16) for i in range(2)]
        r_v = mr(L_R, [128, TT], F32)
        a_v = mr(L_A, [128, TT], F32)
        ig_v = mr(L_IG, [128, TT], F32)
        lru_names = ["ub0", "ub1", "c0", "c1", "c2", "cbf0", "cbf1", "lr", "la", "lig"]
        r2_handoff(lru_names + ["qT", "sq0", "sq1", "sm0", "sm1"])
        ubB = [r2buf("ub0"), r2buf("ub1")]
        cB = [r2buf("c0"), r2buf("c1"), r2buf("c2")]
        cbfB = [r2buf("cbf0"), r2buf("cbf1")]
        rB, aB, igB = r2buf("lr"), r2buf("la"), r2buf("lig")
        if first:
            T.op(DVE, lambda: nc.vector.memset(hstate[:], 0.0), w=[hstB])
            T.op(DVE, lambda: nc.vector.memset(halo[:], 0.0), w=[haloB])
        dense_pool = [0, 1, 2, 3]
        gate_pool = [4, 5]
        lw = {}

        def S1(h):
            blk, c = divmod(h, 4)
            if c == 0:
                Wg, Wgb = load_w("in", CB_LRUG + blk)
                for cc in range(4):
                    pb = next_bank(dense_pool)
                    dense_ws(Wg, Wgb, cc * 128, hT, [hTB], KC, pb)
                    hh = blk * 4 + cc
                    T.op(ACT, lambda: nc.scalar.activation(out=yaT[:, hh, :], in_=bank(pb), func=AF.Gelu),
                         r=[bankB[pb]], w=[yaBc[hh]])
                lw["x"] = load_w("in", CB_LRUX + blk)
            Wx, Wxb = lw["x"]
            pb = next_bank(dense_pool)
            dense_ws(Wx, Wxb, c * 128, hT, [hTB], KC, pb)
            lw[("pb", h)] = pb

        def S1b(h):
            p = h % 2
            pb = lw[("pb", h)]
            T.op(DVE, lambda: nc.vector.tensor_copy(out=ub_v[p][:, 3:515], in_=bank(pb)), r=[bankB[pb]], w=[ubB[p]])

        def S2(h):
            p, q = h % 2, h % 3
            T.op(DVE, lambda: nc.vector.tensor_copy(out=ub_v[p][:, 0:3], in_=halo[:, h, :]), r=[haloB], w=[ubB[p]])
            T.op(DVE, lambda: nc.vector.tensor_scalar(
                out=c_v[q], in0=ub_v[p][:, 0:512], scalar1=cv[:, CV_CONVW + h:CV_CONVW + h + 1],
                scalar2=cv[:, CV_CONVB + h:CV_CONVB + h + 1], op0=ALU.mult, op1=ALU.add), r=[ubB[p], cvB], w=[cB[q]])
            for k in range(1, 4):
                T.op(DVE, lambda: nc.vector.scalar_tensor_tensor(
                    out=c_v[q], in0=ub_v[p][:, k:k + 512], scalar=cv[:, CV_CONVW + 16 * k + h:CV_CONVW + 16 * k + h + 1],
                    in1=c_v[q], op0=ALU.mult, op1=ALU.add), r=[ubB[p], cvB, cB[q]], w=[cB[q]])
            T.op(DVE, lambda: nc.vector.tensor_copy(out=halo[:, h, :], in_=ub_v[p][:, 512:515]), r=[ubB[p]], w=[haloB])

        def S3a(h):
            p, q = h % 2, h % 3
            T.op(DVE, lambda: nc.vector.tensor_copy(out=cbf_v[p], in_=c_v[q]), r=[cB[q]], w=[cbfB[p]])

        def S3b(h):
            p = h % 2
            T.group(PE, [lambda: nc.tensor.matmul(bank(4), lhsT=lruw[:, 0, h, :], rhs=cbf_v[p], start=True, stop=True)],
                    r=[lruwB, cbfB[p]], w=[bankB[4]])
            T.group(PE, [lambda: nc.tensor.matmul(bank(5), lhsT=lruw[:, 1, h, :], rhs=cbf_v[p], start=True, stop=True)],
                    r=[lruwB, cbfB[p]], w=[bankB[5]])

        def S4(h):
            q = h % 3
            T.op(ACT, lambda: nc.scalar.activation(out=r_v, in_=bank(4), func=AF.Tanh,
                                                   bias=dv[:, DV_HBA + h:DV_HBA + h + 1], scale=0.5), r=[bankB[4], dvB], w=[rB])
            T.op(ACT, lambda: nc.scalar.activation(out=ig_v, in_=bank(5), func=AF.Tanh,
                                                   bias=dv[:, DV_HBX + h:DV_HBX + h + 1], scale=0.5), r=[bankB[5], dvB], w=[igB])
            T.op(ACT, lambda: nc.scalar.activation(out=a_v, in_=r_v, func=AF.Exp, scale=dv[:, DV_NSPH + h:DV_NSPH + h + 1],
                                                   bias=dv[:, DV_NSPH + h:DV_NSPH + h + 1]), r=[rB, dvB], w=[aB])
            T.op(ACT, lambda: nc.scalar.activation(out=r_v, in_=r_v, func=AF.Exp, scale=dv[:, DV_NSP + h:DV_NSP + h + 1],
                                                   bias=dv[:, DV_NSP + h:DV_NSP + h + 1]), r=[rB, dvB], w=[rB])
            T.op(ACT, lambda: nc.scalar.activation(out=r_v, in_=r_v, func=AF.Sqrt, bias=1.0, scale=-1.0), r=[rB], w=[rB])
            if first:
                T.op(DVE, lambda: nc.vector.memset(r_v[:, 0:1], 1.0), r=[rB], w=[rB])
            T.op(DVE, lambda: nc.vector.scalar_tensor_tensor(out=ig_v, in0=ig_v, scalar=1.0, in1=c_v[q],
                                                             op0=ALU.add, op1=ALU.mult), r=[igB, cB[q]], w=[igB])
            T.op(DVE, lambda: nc.vector.scalar_tensor_tensor(out=ig_v, in0=ig_v, scalar=0.5, in1=r_v,
                                                             op0=ALU.mult, op1=ALU.mult), r=[igB, rB], w=[igB])
            T.op(DVE, lambda: nc.vector.tensor_tensor_scan(
                out=c_v[q], data0=a_v, data1=ig_v, initial=hstate[:, h:h + 1], op0=ALU.mult, op1=ALU.add),
                r=[aB, igB, hstB, cB[q]], w=[cB[q]])
            T.op(DVE, lambda: nc.vector.tensor_copy(out=hstate[:, h:h + 1], in_=c_v[q][:, 511:512]), r=[cB[q]], w=[hstB])
            T.op(DVE, lambda: nc.vector.tensor_tensor(out=yaT[:, h, :], in0=c_v[q], in1=yaT[:, h, :], op=ALU.mult),
                 r=[cB[q], yaBc[h]], w=[yaBc[h]])

        for t in range(16 + 3):
            if 0 <= t - 2 < 16:
                S3a(t - 2)
            if 0 <= t - 1 < 16:
                S2(t - 1)
            if 0 <= t - 3 < 16:
                S4(t - 3)
            if t < 16:
                S1(t)
            if 0 <= t - 2 < 16:
                S3b(t - 2)
            if t < 16:
                S1b(t)
        if last_dbg:
            dump("yaT", yaT, [128, 16, TT], BF16, yaBc)

        A_Q = 0
        A_E = 8192
        A_SP = A_E + 12288
        A_T = A_SP + 4096
        A_W = A_T + 8192
        A_ACC = A_W + 4096
        assert A_ACC + 2048 <= 40960
        A_SQ = A_W
        A_SM = A_W + 2048
        qT = mr(A_Q, [128, 8, TT], BF16)
        sq_v = [mr(A_SQ + i * 1024, [128, TT], BF16) for i in range(2)]
        sm_v = [mr(A_SM + i * 2048, [128, TT], F32) for i in range(2)]
        e_v = [mr(A_E + i * 4096, [128, 2, TT], F32) for i in range(3)]
        sp_v = [mr(A_SP + i * 2048, [128, 2, TT], BF16) for i in range(2)]
        t_v = [mr(A_T + i * 4096, [128, 2, TT], F32) for i in range(2)]
        w_v = [mr(A_W + i * 2048, [128, 2, TT], BF16) for i in range(2)]
        acc_v = mr(A_ACC, [128, 2, TT], BF16)
        T.handoff([r2buf(nm) for nm in lru_names],
                  [r2buf(f"ae{i}") for i in range(3)] + [r2buf(f"asp{i}") for i in range(2)] + [r2buf(f"at{i}") for i in range(2)])
        qTB_ = r2buf("qT")
        sqB = [r2buf("sq0"), r2buf("sq1")]
        smB = [r2buf("sm0"), r2buf("sm1")]
        eB = [r2buf(f"ae{i}") for i in range(3)]
        spB = [r2buf(f"asp{i}") for i in range(2)]
        tB = [r2buf(f"at{i}") for i in range(2)]
        wB = [r2buf(f"aw{i}") for i in range(2)]
        accB = r2buf("acc")
        nrm = {"i": 0}

        def qk_norm(Wap, Wb, c, gcol_ap, out_ap, outB):
            i = nrm["i"] % 2
            nrm["i"] += 1
            pb = next_bank(dense_pool)
            dense_ws(Wap, Wb, c * 128, hT, [hTB], KC, pb)
            T.op(ACT, lambda: nc.scalar.activation(out=sq_v[i], in_=bank(pb), func=AF.Square), r=[bankB[pb]], w=[sqB[i]])
            pm = next_bank(gate_pool)
            T.group(PE, [lambda: nc.tensor.matmul(bank(pm), lhsT=cm(C_B64), rhs=sq_v[i], start=True, stop=True)],
                    r=[sqB[i], cstB], w=[bankB[pm]])
            T.op(ACT, lambda: nc.scalar.activation(out=sm_v[i], in_=bank(pm), func=AF.Ln, bias=EPS, scale=1.0),
                 r=[bankB[pm]], w=[smB[i]])
            T.op(ACT, lambda: nc.scalar.activation(out=sm_v[i], in_=sm_v[i], func=AF.Exp, scale=-0.5), r=[smB[i]], w=[smB[i]])
            T.op(DVE, lambda: nc.vector.scalar_tensor_tensor(out=out_ap, in0=bank(pb), scalar=gcol_ap, in1=sm_v[i],
                                                             op0=ALU.mult, op1=ALU.mult),
                 r=[bankB[pb], smB[i], cvB, dvB], w=[outB])

        for i in range(2):
            Wk, Wkb = load_w("in", CB_K + i)
            for c in range(4):
                qk_norm(Wk, Wkb, c, cv[:, CV_GK:CV_GK + 1], kT[:, i * 4 + c, qt * TT:(qt + 1) * TT], kTB)
        for i in range(2):
            Wv, Wvb = load_w("in", CB_V + i)
            for s in range(NSUB):
                pb = next_bank(dense_pool)
                dense_as(Wv, Wvb, hT, [hTB], KC, s, pb)
                T.op(ACT, lambda: nc.scalar.activation(out=vC[:, qt * 4 + s, i * 512:(i + 1) * 512],
                                                       in_=bank(pb), func=AF.Copy), r=[bankB[pb]], w=[vCB])
        for i in range(2):
            Wq, Wqb = load_w("in", CB_Q + i)
            for c in range(4):
                qk_norm(Wq, Wqb, c, dv[:, DV_GQ8:DV_GQ8 + 1], qT[:, i * 4 + c, :], qTB_)
        if last_dbg:
            dump("qT", qT, [128, 8, TT], BF16, [qTB_])
            dump("kT", kT[:], [128, 8, SEQ], BF16, [kTB])
            dump("vC", vC[:], [128, SEQ // 128, 1024], BF16, [vCB])

        nkb = 4 * qt + 4
        ZB = [0, 1]
        AB = [2, 3]
        o_pool = [4, 5]

        def geom(n):
            kb = nkb - 1 - n
            di = kb - 4 * qt
            c0 = max(di, 0) * 128
            return kb, di, c0, TT - c0

        T.handoff(sqB + smB, wB + [accB])
        mask2 = cm(C_MASK).unsqueeze(1).to_broadcast([128, 2, 128])

        steps = [(c, n) for c in range(8) for n in range(nkb)]
        NS = len(steps)
        po_of = [next_bank(o_pool) for c in range(8)]

        def pe_z(g):
            c, n = steps[g]
            kb, di, c0, nco = geom(n)
            for e in range(2):
                pr = slice(e * 64, (e + 1) * 64)
                T.group(PE, [lambda: nc.tensor.matmul(bank(e)[:, 0:nco], lhsT=kT[pr, c, kb * 128:(kb + 1) * 128],
                                                      rhs=qT[pr, c, c0:TT], start=True, stop=True)],
                        r=[kTB, qTB_], w=[bankB[e]])

        def pe_cum(g):
            c, n = steps[g]
            kb, di, c0, nco = geom(n)
            i2 = g % 2
            for e in range(2):
                pa = 2 + e
                fns = [lambda: nc.tensor.matmul(bank(pa)[:, 0:nco], lhsT=cm(C_NUINCL), rhs=sp_v[i2][:, e, 0:nco],
                                                start=True, stop=(n == 0))]
                if n > 0:
                    fns.append(lambda: nc.tensor.matmul(bank(pa)[:, 0:nco], lhsT=cm(C_NEGONES), rhs=acc_v[:, e, c0:TT],
                                                        start=False, stop=True))
                T.group(PE, fns, r=[spB[i2], accB, cstB], w=[bankB[pa]])

        def pe_pv(g):
            c, n = steps[g]
            kb, di, c0, nco = geom(n)
            i2 = g % 2
            po = po_of[c]
            for e in range(2):
                pr = slice(e * 64, (e + 1) * 64)
                hd = 2 * c + e
                T.group(PE, [lambda: nc.tensor.matmul(bank(po)[pr, c0:TT], lhsT=vC[:, kb, hd * 64:(hd + 1) * 64],
                                                      rhs=w_v[i2][:, e, 0:nco], start=(n == 0), stop=(n == nkb - 1),
                                                      skip_group_check=True)],
                        r=[vCB, wB[i2]], w=[bankB[po]])
            if n == nkb - 1:
                T.op(DVE, lambda: nc.vector.tensor_copy(out=ybT[:, c, :], in_=bank(po)), r=[bankB[po]], w=[ybB])

        def act_z(g):
            c, n = steps[g]
            kb, di, c0, nco = geom(n)
            i3, i2 = g % 3, g % 2
            T.op(ACT, lambda: nc.scalar.activation(out=e_v[i3][:, :, 0:nco], in_=PS[:, 0:2, 0:nco], func=AF.Exp),
                 r=[bankB[0], bankB[1]], w=[eB[i3]])
            T.op(ACT, lambda: nc.scalar.activation(out=sp_v[i2][:, :, 0:nco], in_=e_v[i3][:, :, 0:nco], func=AF.Ln,
                                                   bias=1.0, scale=1.0), r=[eB[i3]], w=[spB[i2]])

        def act_t(g):
            c, n = steps[g]
            kb, di, c0, nco = geom(n)
            i2 = g % 2
            T.op(ACT, lambda: nc.scalar.activation(out=t_v[i2][:, :, 0:nco], in_=PS[:, 2:4, 0:nco], func=AF.Exp),
                 r=[bankB[2], bankB[3]], w=[tB[i2]])

        def dve_w(g):
            c, n = steps[g]
            kb, di, c0, nco = geom(n)
            i3, i2 = g % 3, g % 2
            T.op(DVE, lambda: nc.vector.tensor_tensor(out=w_v[i2][:, :, 0:nco], in0=e_v[i3][:, :, 0:nco],
                                                      in1=t_v[i2][:, :, 0:nco], op=ALU.mult),
                 r=[eB[i3], tB[i2]], w=[wB[i2]])
            if di >= 0:
                T.op(DVE, lambda: nc.vector.tensor_tensor(out=w_v[i2][:, :, 0:128], in0=w_v[i2][:, :, 0:128],
                                                          in1=mask2, op=ALU.mult), r=[wB[i2], cstB], w=[wB[i2]])

        def dve_acc(g):
            c, n = steps[g]
            kb, di, c0, nco = geom(n)
            i2 = g % 2
            if n == 0:
                T.op(DVE, lambda: nc.vector.memset(acc_v, 0.0), w=[accB])
            if n < nkb - 1:
                T.op(DVE, lambda: nc.vector.tensor_tensor(out=acc_v[:, :, c0:TT], in0=acc_v[:, :, c0:TT],
                                                          in1=sp_v[i2][:, :, 0:nco], op=ALU.add),
                     r=[accB, spB[i2]], w=[accB])

        def dve_mask_sp(g):
            c, n = steps[g]
            kb, di, c0, nco = geom(n)
            i2 = g % 2
            if di >= 0:
                T.op(DVE, lambda: nc.vector.tensor_tensor(out=sp_v[i2][:, :, 0:128], in0=sp_v[i2][:, :, 0:128],
                                                          in1=mask2, op=ALU.mult), r=[spB[i2], cstB], w=[spB[i2]])

        for g in range(NS + 3):
            if g < NS:
                pe_z(g)
            if 0 <= g - 3 < NS:
                pe_pv(g - 3)
            if 0 <= g - 1 < NS:
                pe_cum(g - 1)
            if g < NS:
                act_z(g)
            if 0 <= g - 1 < NS:
                act_t(g - 1)
            if 0 <= g - 2 < NS:
                dve_w(g - 2)
            if 0 <= g - 1 < NS:
                dve_acc(g - 1)
            if g < NS:
                dve_mask_sp(g)
        if last_dbg:
            dump("ybT", ybT, [128, 8, TT], BF16, [ybB])


        D_Q = 0
        D_SQ = 8192
        D_SM = D_SQ + 2048
        D_EM = D_SM + 2048
        D_RD = D_EM + 4096
        assert D_RD + 4096 <= 24576
        qmT = r2(D_Q, [128, 8, TT], BF16)
        msq_v = [r2(D_SQ + i * 1024, [128, TT], BF16) for i in range(2)]
        msm_v = r2(D_SM, [128, TT], F32)
        em_v = [r2(D_EM + i * 1024, [128, TT], BF16) for i in range(4)]
        rd_v = [r2(D_RD + i * 2048, [128, TT], F32) for i in range(2)]
        names = ["qmT", "dsq0", "dsq1", "dsm", "em0", "em1", "em2", "em3", "rd0", "rd1"]
        r2_handoff(names)
        qmB = r2buf("qmT")
        dsqB = [r2buf("dsq0"), r2buf("dsq1")]
        dsmB = r2buf("dsm")
        emB = [r2buf(f"em{i}") for i in range(4)]
        rdB = [r2buf("rd0"), r2buf("rd1")]
        for i in range(2):
            Wq, Wqb = load_w("in", CB_QM + i)
            for hp in range(2):
                hm = 2 * i + hp
                pbs = [2 * hp, 2 * hp + 1]
                for e in range(2):
                    dense_ws(Wq, Wqb, (2 * hp + e) * 128, hT, [hTB], KC, pbs[e])
                    T.op(ACT, lambda e=e: nc.scalar.activation(out=msq_v[e], in_=bank(pbs[e]), func=AF.Square),
                         r=[bankB[pbs[e]]], w=[dsqB[e]])
                pm = next_bank(gate_pool)
                T.group(PE, [lambda: nc.tensor.matmul(bank(pm), lhsT=cm(C_O256), rhs=msq_v[0], start=True, stop=False),
                             lambda: nc.tensor.matmul(bank(pm), lhsT=cm(C_O256), rhs=msq_v[1], start=False, stop=True)],
                        r=dsqB + [cstB], w=[bankB[pm]])
                T.op(ACT, lambda: nc.scalar.activation(out=msm_v, in_=bank(pm), func=AF.Ln, bias=EPS, scale=1.0),
                     r=[bankB[pm]], w=[dsmB])
                T.op(ACT, lambda: nc.scalar.activation(out=msm_v, in_=msm_v, func=AF.Exp, scale=-0.5), r=[dsmB], w=[dsmB])
                for e in range(2):
                    T.op(DVE, lambda e=e: nc.vector.scalar_tensor_tensor(
                        out=qmT[:, 2 * hm + e, :], in0=bank(pbs[e]), scalar=dv[:, DV_MGQ + e:DV_MGQ + e + 1],
                        in1=msm_v, op0=ALU.mult, op1=ALU.mult), r=[bankB[pbs[e]], dsmB, dvB], w=[qmB])
        if last_dbg:
            dump("qmT", qmT, [128, 8, TT], BF16, [qmB])
        z_pool = [0, 1]
        a_pool = [2, 3]
        for hm in range(4):
            ei = (hm % 2) * 2
            for mb in range(2):
                pz = next_bank(z_pool)
                T.group(PE, [lambda e=e: nc.tensor.matmul(bank(pz), lhsT=kmT[:, 2 * hm + e, mb * 128:(mb + 1) * 128],
                                                          rhs=qmT[:, 2 * hm + e, :], start=(e == 0), stop=(e == 1))
                             for e in range(2)], r=[kmTB, qmB], w=[bankB[pz]])
                T.op(ACT, lambda mb=mb: nc.scalar.activation(out=em_v[ei + mb], in_=bank(pz), func=AF.Exp),
                     r=[bankB[pz]], w=[emB[ei + mb]])
            pd = next_bank(a_pool)
            T.group(PE, [lambda mb=mb: nc.tensor.matmul(bank(pd), lhsT=cm(C_ONES), rhs=em_v[ei + mb], start=(mb == 0), stop=(mb == 1))
                         for mb in range(2)], r=[emB[ei], emB[ei + 1], cstB], w=[bankB[pd]])
            ri = hm % 2
            T.op(DVE, lambda: nc.vector.reciprocal(out=rd_v[ri], in_=bank(pd)), r=[bankB[pd]], w=[rdB[ri]])
            for e in range(2):
                po = next_bank(o_pool)
                T.group(PE, [lambda mb=mb: nc.tensor.matmul(bank(po), lhsT=vm[:, mb, (2 * hm + e) * 128:(2 * hm + e + 1) * 128],
                                                            rhs=em_v[ei + mb], start=(mb == 0), stop=(mb == 1))
                             for mb in range(2)], r=[vmB, emB[ei], emB[ei + 1]], w=[bankB[po]])
                T.op(DVE, lambda: nc.vector.tensor_tensor(out=ycT[:, 2 * hm + e, :], in0=bank(po), in1=rd_v[ri], op=ALU.mult),
                     r=[bankB[po], rdB[ri]], w=[ycB])
        if last_dbg:
            dump("ycT", ycT, [128, 8, TT], BF16, [ycB])

        E_SIG = 0
        E_ACC = 8192
        E_TMP = 16384
        sig_v = [r2(E_SIG + i * 2048, [128, TT], F32) for i in range(4)]
        macc_v = [r2(E_ACC + i * 2048, [128, TT], F32) for i in range(4)]
        tmp_v = [r2(E_TMP + i * 2048, [128, TT], F32) for i in range(2)]
        names = [f"sig{i}" for i in range(4)] + [f"macc{i}" for i in range(4)] + ["mtmp0", "mtmp1", "mergedT"]
        r2_handoff(names)
        sigB = [r2buf(f"sig{i}") for i in range(4)]
        maccB = [r2buf(f"macc{i}") for i in range(4)]
        tmpB = [r2buf("mtmp0"), r2buf("mtmp1")]
        allp = [0, 1, 2, 3, 4, 5]
        ti = 0
        for cbo in range(4):
            for br in range(3):
                Wg, Wgb = load_w("in", CB_GATE + br * 4 + cbo)
                for c in range(4):
                    pb = next_bank(allp)
                    dense_ws(Wg, Wgb, c * 128, hT, [hTB], KC, pb)
                    col = CV_BGATE + br * 16 + cbo * 4 + c
                    T.op(ACT, lambda pb=pb, c=c, col=col: nc.scalar.activation(out=sig_v[c], in_=bank(pb), func=AF.Sigmoid,
                                                                               bias=cv[:, col:col + 1], scale=1.0),
                         r=[bankB[pb], cvB], w=[sigB[c]])
                if br == 0:
                    Wp, Wpb = load_w("pa", cbo)
                    src, srcB, nk = yaT, yaBc, 16
                elif br == 1:
                    Wp, Wpb = load_w("pb", cbo, 0, 8)
                    src, srcB, nk = ybT, [ybB], 8
                else:
                    Wp, Wpb = load_w("pc", cbo, 0, 8)
                    src, srcB, nk = ycT, [ycB], 8
                for c in range(4):
                    pb = next_bank(allp)
                    dense_ws(Wp, Wpb, c * 128, src, srcB, nk, pb)
                    if br == 0:
                        T.op(DVE, lambda pb=pb, c=c: nc.vector.tensor_tensor(out=macc_v[c], in0=bank(pb), in1=sig_v[c], op=ALU.mult),
                             r=[bankB[pb], sigB[c]], w=[maccB[c]])
                    else:
                        t = ti % 2
                        ti += 1
                        T.op(DVE, lambda pb=pb, c=c, t=t: nc.vector.tensor_tensor(out=tmp_v[t], in0=bank(pb), in1=sig_v[c], op=ALU.mult),
                             r=[bankB[pb], sigB[c]], w=[tmpB[t]])
                        if br == 1:
                            T.op(DVE, lambda c=c, t=t: nc.vector.tensor_tensor(out=macc_v[c], in0=macc_v[c], in1=tmp_v[t], op=ALU.add),
                                 r=[maccB[c], tmpB[t]], w=[maccB[c]])
                        else:
                            T.op(DVE, lambda c=c, t=t: nc.vector.tensor_tensor(out=mergedT[:, cbo * 4 + c, :], in0=macc_v[c],
                                                                                in1=tmp_v[t], op=ALU.add),
                                 r=[maccB[c], tmpB[t]], w=[mgB])
        if last_dbg:
            dump("mergedT", mergedT, [128, 16, TT], BF16, [mgB])

        T.handoff(yaBc[0:8], [x1B[0]])
        T.handoff(yaBc[8:16], [x1B[1]])
        T.handoff([ybB], [x1B[2]])
        T.handoff([ycB], [x1B[3]])
        for s in range(NSUB):
            T.dma(SP, x1[:, s, :], x_d[tok0 + s * 128: tok0 + (s + 1) * 128, :], x1sem[s], w=[x1B[s]])
        for cb in range(4):
            Wo, Wob = load_w("out", cb)
            for s in range(NSUB):
                pb = next_bank(allp)
                dense_as(Wo, Wob, mergedT, [mgB], KC, s, pb)
                T.op(DVE, lambda pb=pb, s=s, cb=cb: nc.vector.tensor_tensor(
                    out=x1[:, s, cb * 512:(cb + 1) * 512], in0=bank(pb), in1=x1[:, s, cb * 512:(cb + 1) * 512], op=ALU.add),
                    r=[bankB[pb], x1B[s]], w=[x1B[s]])
        if last_dbg:
            dump("x1", x1, [128, NSUB, D], F32, x1B)

        r2_handoff(["xs0", "xs1", "xs2", "xs3", "xn0", "xn1", "xn2", "xn3"])
        norm_transpose([x1[:, s, :] for s in range(NSUB)], NSUB, CV_GFFN, hT, hTB, None, 0, load=False,
                       srcs=[[x1B[s]] for s in range(NSUB)])

        H_ACT = 0
        H_SG = 24576
        actg = [view(MR2, H_ACT + i * 12288, [128, 12, TT], BF16) for i in range(2)]
        sg_v = [view(MR2, H_SG + i * 2048, [128, TT], F32) for i in range(8)]
        names = ["actg0", "actg1"] + [f"sg{i}" for i in range(8)]
        r2_handoff(names)
        actB = [r2buf("actg0"), r2buf("actg1")]
        sgB = [r2buf(f"sg{i}") for i in range(8)]
        T.handoff([mgB], actB + sgB)
        groups = [[0, 1, 2], [3, 4, 5], [6, 7, 8], [9, 10]]
        sgi = 0
        for gi_, blks in enumerate(groups):
            ag = actg[gi_ % 2]
            agB = actB[gi_ % 2]
            for bi_, blk in enumerate(blks):
                Wg, Wgb = load_w("fc", blk)
                sidx = []
                for c in range(4):
                    pb = next_bank(allp)
                    dense_ws(Wg, Wgb, c * 128, hT, [hTB], KC, pb)
                    si = sgi % 8
                    sgi += 1
                    sidx.append(si)
                    T.op(ACT, lambda pb=pb, si=si: nc.scalar.activation(out=sg_v[si], in_=bank(pb), func=AF.Silu),
                         r=[bankB[pb]], w=[sgB[si]])
                Wu, Wub = load_w("fc", 11 + blk)
                for c in range(4):
                    pb = next_bank(allp)
                    dense_ws(Wu, Wub, c * 128, hT, [hTB], KC, pb)
                    si = sidx[c]
                    T.op(DVE, lambda pb=pb, si=si, bi_=bi_, c=c: nc.vector.tensor_tensor(
                        out=ag[:, bi_ * 4 + c, :], in0=bank(pb), in1=sg_v[si], op=ALU.mult),
                        r=[bankB[pb], sgB[si]], w=[agB])
            nk = 4 * len(blks)
            k0 = 4 * blks[0]
            for cb in range(4):
                Wd, Wdb = load_w("down", cb, k0, nk)
                for s in range(NSUB):
                    pb = next_bank(allp)
                    dense_as(Wd, Wdb, ag, [agB], nk, s, pb)
                    T.op(DVE, lambda pb=pb, s=s, cb=cb: nc.vector.tensor_tensor(
                        out=x1[:, s, cb * 512:(cb + 1) * 512], in0=bank(pb), in1=x1[:, s, cb * 512:(cb + 1) * 512], op=ALU.add),
                        r=[bankB[pb], x1B[s]], w=[x1B[s]])
        for s in range(NSUB):
            T.dma(SP, y_d[tok0 + s * 128: tok0 + (s + 1) * 128, :], x1[:, s, :], osem[s], r=[x1B[s]])
        T.handoff(x1B, yaBc + [ybB, ycB])
        T.handoff(actB, [mgB])

    for b in range(NSEQ):
        mem_prep(b)
        for qt in range(NT):
            tile(b, qt)

    for s in range(NSUB):
        SP.h.wait_ge(osem[s][0], osem[s][1])
    if dbg_sem[1]:
        SP.h.wait_ge(dbg_sem[0], dbg_sem[1])
    stats = {e.name: (e.nops, e.nwaits) for e in (PE, ACT, DVE, POOL, SP)}
    return nc, dbg_outs, stats


def make_consts():
    c = np.zeros((128, NCONST, 128), np.float32)
    idx = np.arange(128)
    c[:, C_IDENT, :] = np.eye(128, dtype=np.float32)
    c[:, C_MASK, :] = (idx[:, None] < idx[None, :]).astype(np.float32)
    c[:, C_NUINCL, :] = -(idx[:, None] >= idx[None, :]).astype(np.float32)
    c[:, C_NEGONES, :] = -1.0
    blk = (idx[:, None] // 64 == idx[None, :] // 64).astype(np.float32) / 64.0
    c[:, C_B64, :] = blk
    c[:, C_O256, :] = 1.0 / 256.0
    c[:, C_ONES, :] = 1.0
    return c


def pack_cvec(g_mix, g_mem, b_gate, conv_w, conv_b, lru_ba, lru_bx, lru_lambda, sb_gq, sb_gk, mem_gq, mem_gk, g_ffn):
    cvec = np.zeros((128, CV_N), np.float32)
    cvec[:, CV_GMIX:CV_GMIX + 16] = g_mix.reshape(16, 128).T
    cvec[:, CV_GFFN:CV_GFFN + 16] = g_ffn.reshape(16, 128).T
    cvec[:, CV_GMEM:CV_GMEM + 16] = g_mem.reshape(16, 128).T
    for k in range(4):
        cvec[:, CV_CONVW + 16 * k:CV_CONVW + 16 * (k + 1)] = conv_w[k].reshape(16, 128).T
    cvec[:, CV_CONVB:CV_CONVB + 16] = conv_b.reshape(16, 128).T
    cvec[:, CV_BA:CV_BA + 16] = lru_ba.reshape(16, 128).T
    cvec[:, CV_BX:CV_BX + 16] = lru_bx.reshape(16, 128).T
    cvec[:, CV_LAM:CV_LAM + 16] = lru_lambda.reshape(16, 128).T
    cvec[:, CV_BGATE:CV_BGATE + 48] = b_gate.reshape(48, 128).T
    cvec[:, CV_GQ] = np.tile(sb_gq, 2)
    cvec[:, CV_GK] = np.tile(sb_gk, 2)
    cvec[:, CV_MGQ:CV_MGQ + 2] = mem_gq.reshape(2, 128).T
    cvec[:, CV_MGK:CV_MGK + 2] = mem_gk.reshape(2, 128).T
    return cvec


_NC_CACHE = {}


def kernel(x, mem, g_mix, g_mem, w_in, b_gate, conv_w, conv_b, lru_wa, lru_ba, lru_wx, lru_bx,
           lru_lambda, sb_gq, sb_gk, mem_w_kv, mem_gq, mem_gk, w_pa, w_pb, w_pc, w_out,
           g_ffn, w_fc, w_down):
    n = 8
    x = np.asarray(x, np.float32)
    mem = np.asarray(mem, np.float32)
    Bn, Tn, Dn = x.shape
    nseq = Bn // n
    f = lambda a: np.ascontiguousarray(np.asarray(a, np.float32))
    cvec = pack_cvec(f(g_mix)[0], f(g_mem)[0], f(b_gate)[0], f(conv_w)[0], f(conv_b)[0], f(lru_ba)[0], f(lru_bx)[0],
                     f(lru_lambda)[0], f(sb_gq)[0], f(sb_gk)[0], f(mem_gq)[0], f(mem_gk)[0], f(g_ffn)[0])
    consts = make_consts()
    lru_w = np.ascontiguousarray(np.stack([f(lru_wa)[0], f(lru_wx)[0]], axis=0))
    shared = {
        "w_in": f(w_in)[0], "w_kv": f(mem_w_kv)[0], "w_pa": f(w_pa)[0], "w_pb": f(w_pb)[0], "w_pc": f(w_pc)[0],
        "w_out": f(w_out)[0], "w_fc": f(w_fc)[0], "w_down": f(w_down)[0], "lru_w": lru_w, "cvec": cvec, "consts": consts,
    }
    key = (nseq, Tn)
    if key not in _NC_CACHE:
        _NC_CACHE[key] = build_nc(NSEQ=nseq, SEQ=Tn)[0]
    nc = _NC_CACHE[key]
    in_maps = []
    for c in range(n):
        m = dict(shared)
        m["x"] = np.ascontiguousarray(x[c * nseq:(c + 1) * nseq].reshape(nseq * Tn, Dn))
        m["mem"] = np.ascontiguousarray(mem[c * nseq:(c + 1) * nseq].reshape(nseq * MEM, Dn))
        in_maps.append(m)
    res = run_bass_kernel_spmd(nc, in_maps, core_ids=list(range(n)))
    out = np.concatenate([r["y"].reshape(nseq, Tn, Dn) for r in res.results], axis=0)
    return out.astype(np.float32)
```

```python
import numpy as np
import concourse.bass as bass
import concourse.mybir as mybir
from concourse.bass_utils import run_bass_kernel_spmd

F32 = mybir.dt.float32
BF16 = mybir.dt.bfloat16
U8 = mybir.dt.uint8
AF = mybir.ActivationFunctionType
ALU = mybir.AluOpType

D = 2048
KC = D // 128
TT = 512
NSUB = TT // 128
MEM = 256
DFF = 5632
NFF = DFF // 128
EPS = 1e-6
IN_COLS = 14336

CB_LRUX = 0
CB_LRUG = 4
CB_Q = 8
CB_K = 10
CB_V = 12
CB_QM = 14
CB_GATE = 16

CV_GMIX = 0
CV_GFFN = 16
CV_GMEM = 32
CV_CONVW = 48
CV_CONVB = 112
CV_BA = 128
CV_BX = 144
CV_LAM = 160
CV_BGATE = 176
CV_GQ = 224
CV_GK = 225
CV_MGQ = 226
CV_MGK = 228
CV_N = 230

C_IDENT = 0
C_MASK = 1
C_NUINCL = 2
C_NEGONES = 3
C_B64 = 4
C_O256 = 5
C_ONES = 6
NCONST = 7


class Buf:
    __slots__ = ("name", "w", "r")

    def __init__(self, name):
        self.name = name
        self.w = []
        self.r = []


class Eng:
    def __init__(self, nc, h, name, is_pe=False):
        self.h = h
        self.name = name
        self.sem = nc.alloc_semaphore("clk_" + name)
        self.cnt = 0
        self.seen = {}
        self.is_pe = is_pe
        self.nwaits = 0
        self.nops = 0


class Tracker:
    def __init__(self, nc):
        self.nc = nc
        self.pe = Eng(nc, nc.tensor, "pe", is_pe=True)
        self.act = Eng(nc, nc.scalar, "act")
        self.dve = Eng(nc, nc.vector, "dve")
        self.pool = Eng(nc, nc.gpsimd, "pool")
        self.sp = Eng(nc, nc.sync, "sp")

    def _collect(self, E, r, w):
        toks = []
        for b in r:
            for t in b.w:
                if t[0] is E.sem and E.is_pe:
                    continue
                toks.append(t)
        for b in w:
            for t in b.w:
                if t[0] is E.sem and E.is_pe:
                    continue
                toks.append(t)
            for t in b.r:
                if t[0] is E.sem and E.is_pe:
                    continue
                toks.append(t)
        return toks

    def _wait(self, E, toks):
        best = {}
        for sem, val in toks:
            k = id(sem)
            if E.seen.get(k, 0) >= val:
                continue
            if k not in best or best[k][1] < val:
                best[k] = (sem, val)
        for k, (sem, val) in best.items():
            E.h.wait_ge(sem, val)
            E.seen[k] = val
            E.nwaits += 1

    def _commit(self, tok, r, w):
        for b in r:
            b.r.append(tok)
            if len(b.r) > 64:
                newest = {}
                for t in b.r:
                    k = id(t[0])
                    if k not in newest or newest[k][1] < t[1]:
                        newest[k] = t
                b.r = list(newest.values())
        for b in w:
            b.w = [tok]
            b.r = []

    def op(self, E, fn, r=(), w=()):
        self._wait(E, self._collect(E, r, w))
        ins = fn()
        E.cnt += 1
        E.nops += 1
        ins.then_inc(E.sem, 1)
        self._commit((E.sem, E.cnt), r, w)
        return ins

    def group(self, E, fns, r=(), w=()):
        self._wait(E, self._collect(E, r, w))
        ins = None
        for fn in fns:
            ins = fn()
            E.nops += 1
        E.cnt += 1
        ins.then_inc(E.sem, 1)
        self._commit((E.sem, E.cnt), r, w)

    def dma(self, E, out, in_, semstate, r=(), w=()):
        self._wait(E, self._collect(E, r, w))
        E.h.dma_start(out=out, in_=in_).then_inc(semstate[0], 16)
        semstate[1] += 16
        self._commit((semstate[0], semstate[1]), r, w)

    def handoff(self, old, new):
        toks = []
        for b in old:
            toks += b.w + b.r
        newest = {}
        for t in toks:
            k = id(t[0])
            if k not in newest or newest[k][1] < t[1]:
                newest[k] = t
        toks = list(newest.values())
        for b in new:
            b.w = list(toks) + b.w
            b.r = list(b.r)


def build_nc(NSEQ=2, SEQ=2048, dbg=None):
    dbg = dbg or {}
    NT = SEQ // TT
    NTOK = NSEQ * SEQ
    nc = bass.Bass("TRN2", target_bir_lowering=False)
    T = Tracker(nc)
    PE, ACT, DVE, POOL, SP = T.pe, T.act, T.dve, T.pool, T.sp

    def dram_in(name, shape):
        return nc.dram_tensor(name, shape, F32, kind="ExternalInput").ap()

    x_d = dram_in("x", [NTOK, D])
    mem_d = dram_in("mem", [NSEQ * MEM, D])
    w_in_d = dram_in("w_in", [D, IN_COLS])
    w_kv_d = dram_in("w_kv", [D, 2048])
    w_pa_d = dram_in("w_pa", [2048, D])
    w_pb_d = dram_in("w_pb", [1024, D])
    w_pc_d = dram_in("w_pc", [1024, D])
    w_out_d = dram_in("w_out", [D, D])
    w_fc_d = dram_in("w_fc", [D, 2 * DFF])
    w_down_d = dram_in("w_down", [DFF, D])
    lru_w_d = dram_in("lru_w", [2, 16, 128, 128])
    cvec_d = dram_in("cvec", [128, CV_N])
    consts_d = dram_in("consts", [128, NCONST, 128])
    y_d = nc.dram_tensor("y", [NTOK, D], F32, kind="ExternalOutput").ap()

    wsrc = {"in": w_in_d, "kv": w_kv_d, "pa": w_pa_d, "pb": w_pb_d, "pc": w_pc_d,
            "out": w_out_d, "fc": w_fc_d, "down": w_down_d}
    wbf = {k: nc.dram_tensor("wb_" + k, list(v.shape), BF16, kind="Internal").ap() for k, v in wsrc.items()}
    wconv = {}
    NCS = 6
    csems = [[nc.alloc_semaphore(f"cv{i}"), 0] for i in range(NCS)]
    cstate = {"n": 0}

    def ensure_conv(mat, cb):
        key = (mat, cb)
        if key in wconv:
            return wconv[key]
        b = Buf(f"wb_{mat}_{cb}")
        i = cstate["n"]
        cstate["n"] += 1
        ss = csems[i % NCS]
        if ss[1] > 0:
            POOL.h.wait_ge(ss[0], ss[1])
        T.dma(POOL, wbf[mat][:, cb * 512:(cb + 1) * 512], wsrc[mat][:, cb * 512:(cb + 1) * 512], ss, w=[b])
        wconv[key] = b
        return b

    def sb(name, shape, dt):
        return nc.alloc_sbuf_tensor(name, shape, dt)

    kT = sb("kT", [128, 8, SEQ], BF16)
    vC = sb("vC", [128, SEQ // 128, 1024], BF16)
    kmT = sb("kmT", [128, 8, MEM], BF16)
    vm = sb("vm", [128, 2, 1024], BF16)
    hT = sb("hT", [128, KC, TT], BF16)
    R1 = sb("R1", [128, 32768], U8)
    MR2 = sb("MR2", [128, 16384 + 24576], U8)
    W = [sb("W0", [128, KC, 512], BF16), sb("W1", [128, KC, 512], BF16)]
    lruw = sb("lruw", [128, 2, 16, 128], BF16)
    cst = sb("cst", [128, NCONST, 128], BF16)
    cv = sb("cv", [128, CV_N], F32)
    dv = sb("dv", [128, 112], F32)
    hstate = sb("hstate", [128, 16], F32)
    halo = sb("halo", [128, 16, 3], F32)
    stat = sb("stat", [128, 16], F32)

    def view(region, off, shape, dt):
        nb = int(np.prod(shape[1:])) * (4 if dt == F32 else 2)
        v = region[:, off:off + nb].bitcast(dt)
        if len(shape) == 3:
            v = v.rearrange("p (a b) -> p a b", a=shape[1])
        return v

    yaT = view(R1, 0, [128, 16, TT], BF16)
    ybT = view(R1, 16384, [128, 8, TT], BF16)
    ycT = view(R1, 24576, [128, 8, TT], BF16)
    x1 = view(R1, 0, [128, NSUB, D], F32)
    mergedT = view(MR2, 0, [128, 16, TT], BF16)
    R2OFF = 16384

    def r2(off, shape, dt):
        return view(MR2, R2OFF + off, shape, dt)

    PS = nc.alloc_psum_tensor("ps", [128, 8, 512], F32)
    bankB = [Buf(f"bank{i}") for i in range(8)]

    def bank(i):
        return PS[:, i, :]

    B = {}

    def buf(name):
        if name not in B:
            B[name] = Buf(name)
        return B[name]

    def dsem(name):
        return [nc.alloc_semaphore(name), 0]

    wsem = [dsem("w0"), dsem("w1")]
    xsem = [dsem(f"x{i}") for i in range(4)]
    osem = [dsem(f"o{s}") for s in range(NSUB)]
    x1sem = [dsem(f"xr{s}") for s in range(NSUB)]
    misc_sem = dsem("misc")
    misc2_sem = dsem("misc2")
    misc3_sem = dsem("misc3")
    dbg_sem = dsem("dbg")

    Wbuf = [Buf("W0"), Buf("W1")]
    wrr = {"i": 0}

    def load_w(mat, cb, k0=0, nk=KC):
        cbuf = ensure_conv(mat, cb)
        s = wrr["i"] % 2
        wrr["i"] += 1
        src = wbf[mat][k0 * 128:(k0 + nk) * 128, cb * 512:(cb + 1) * 512].rearrange("(kc p) c -> p kc c", p=128)
        T.dma(SP, W[s][:, 0:nk, :], src, wsem[s], r=[cbuf], w=[Wbuf[s]])
        return W[s], Wbuf[s]

    cvB, cstB, lruwB, dvB = buf("cv"), buf("cst"), buf("lruw"), buf("dv")
    T.dma(SP, cv[:], cvec_d, misc_sem, w=[cvB])
    T.dma(POOL, cst[:], consts_d, misc2_sem, w=[cstB])
    T.dma(POOL, lruw[:].rearrange("p a h j -> p (a h) j"),
          lru_w_d.rearrange("a h i j -> i (a h) j"), misc3_sem, w=[lruwB])

    def cm(i):
        return cst[:, i, :]

    DV_GQ8, DV_MGQ, DV_NSP, DV_NSP2, DV_TMP, DV_NSPH, DV_HBA, DV_HBX = 0, 1, 16, 32, 48, 64, 80, 96
    T.op(DVE, lambda: nc.vector.tensor_scalar(out=dv[:, DV_GQ8:DV_GQ8 + 1], in0=cv[:, CV_GQ:CV_GQ + 1],
                                              scalar1=0.125, scalar2=None, op0=ALU.mult), r=[cvB], w=[dvB])
    T.op(DVE, lambda: nc.vector.tensor_scalar(out=dv[:, DV_MGQ:DV_MGQ + 2], in0=cv[:, CV_MGQ:CV_MGQ + 2],
                                              scalar1=1.0 / 16.0, scalar2=None, op0=ALU.mult), r=[cvB], w=[dvB])
    T.op(ACT, lambda: nc.scalar.activation(out=dv[:, DV_TMP:DV_TMP + 16], in_=cv[:, CV_LAM:CV_LAM + 16],
                                           func=AF.Exp, scale=-1.0), r=[cvB, dvB], w=[dvB])
    T.op(ACT, lambda: nc.scalar.activation(out=dv[:, DV_TMP:DV_TMP + 16], in_=dv[:, DV_TMP:DV_TMP + 16],
                                           func=AF.Ln, bias=1.0, scale=1.0), r=[dvB], w=[dvB])
    T.op(DVE, lambda: nc.vector.tensor_scalar(out=dv[:, DV_NSP:DV_NSP + 16], in0=dv[:, DV_TMP:DV_TMP + 16],
                                              scalar1=-8.0, scalar2=None, op0=ALU.mult), r=[dvB], w=[dvB])
    T.op(DVE, lambda: nc.vector.tensor_scalar(out=dv[:, DV_NSP2:DV_NSP2 + 16], in0=dv[:, DV_TMP:DV_TMP + 16],
                                              scalar1=-16.0, scalar2=None, op0=ALU.mult), r=[dvB], w=[dvB])
    T.op(DVE, lambda: nc.vector.tensor_scalar(out=dv[:, DV_NSPH:DV_NSPH + 16], in0=dv[:, DV_TMP:DV_TMP + 16],
                                              scalar1=-4.0, scalar2=None, op0=ALU.mult), r=[dvB], w=[dvB])
    T.op(DVE, lambda: nc.vector.tensor_scalar(out=dv[:, DV_HBA:DV_HBA + 16], in0=cv[:, CV_BA:CV_BA + 16],
                                              scalar1=0.5, scalar2=None, op0=ALU.mult), r=[cvB, dvB], w=[dvB])
    T.op(DVE, lambda: nc.vector.tensor_scalar(out=dv[:, DV_HBX:DV_HBX + 16], in0=cv[:, CV_BX:CV_BX + 16],
                                              scalar1=0.5, scalar2=None, op0=ALU.mult), r=[cvB, dvB], w=[dvB])

    dbg_outs = {}

    def dump(name, ap_sb, shape, dt, rbufs):
        if name not in dbg:
            return
        if name in dbg_outs:
            return
        t = nc.dram_tensor("dbg_" + name, shape, dt, kind="ExternalOutput").ap()
        dbg_outs[name] = t
        T.dma(SP, t, ap_sb, dbg_sem, r=rbufs, w=[buf("dbgdram_" + name)])

    psrr = {"i": 0}

    def next_bank(pool):
        i = pool[psrr.setdefault(id(pool), 0) % len(pool)]
        psrr[id(pool)] += 1
        return i

    def dense_ws(Wap, Wb, col0, actT, actB, nk, pb, ncols=TT, c0=0):
        fns = []
        for k in range(nk):
            fns.append(lambda k=k: nc.tensor.matmul(bank(pb)[:, 0:ncols], lhsT=Wap[:, k, col0:col0 + 128],
                                                    rhs=actT[:, k, c0:c0 + ncols], start=(k == 0), stop=(k == nk - 1)))
        T.group(PE, fns, r=[Wb] + actB, w=[bankB[pb]])

    def dense_as(Wap, Wb, actT, actB, nk, s, pb, kofs=0):
        fns = []
        for k in range(nk):
            fns.append(lambda k=k: nc.tensor.matmul(bank(pb), lhsT=actT[:, kofs + k, s * 128:(s + 1) * 128],
                                                    rhs=Wap[:, k, :], start=(k == 0), stop=(k == nk - 1)))
        T.group(PE, fns, r=[Wb] + actB, w=[bankB[pb]])

    PST = PS[:, 6:8, :].bitcast(BF16).rearrange("p a (b c) -> p (a b) c", c=128)

    PST2 = [PST, PS[:, 4:6, :].bitcast(BF16).rearrange("p a (b c) -> p (a b) c", c=128)]
    PSTB = [[bankB[6], bankB[7]], [bankB[4], bankB[5]]]

    def norm_transpose(src_rows, nrows_tiles, gcol, dstT, dstB, srcB_list, stage_off, load=True, srcs=None, nsub_cols=None):
        n = len(src_rows)
        if load:
            xs_v = [view(MR2, i * 8192, [128, D], F32) for i in range(4)]
            xn_v = [view(MR2, 32768 + i * 4096, [128, D], BF16) for i in range(2)]
        else:
            xs_v = []
            xn_v = [view(MR2, 24576 + i * 4096, [128, D], BF16) for i in range(4)]
        nxn = len(xn_v)
        xsB = [buf(f"xs{i}") for i in range(4)]
        xnB = [buf(f"xn{i}") for i in range(4)]
        stB = [buf(f"stat{i}") for i in range(4)]
        xin, xinB = [], []
        for s, src in enumerate(src_rows):
            if load:
                T.dma(SP, xs_v[s], src, xsem[s], w=[xsB[s]])
                xin.append(xs_v[s])
                xinB.append([xsB[s]])
            else:
                xin.append(src)
                xinB.append(srcs[s])
        for s in range(n):
            c = 2 * s
            T.op(DVE, lambda: nc.vector.memset(stat[:, c:c + 2], 0.0), w=[stB[s]])
            T.op(ACT, lambda: nc.scalar.activation(out=xn_v[s % nxn], in_=xin[s], func=AF.Square,
                                                   accum_out=stat[:, c:c + 1]), r=xinB[s] + [stB[s]], w=[xnB[s % nxn], stB[s]])
        for s in range(n):
            c = 2 * s
            T.op(ACT, lambda: nc.scalar.activation(out=stat[:, c + 1:c + 2], in_=stat[:, c:c + 1], func=AF.Sqrt,
                                                   bias=EPS, scale=1.0 / D), r=[stB[s]], w=[stB[s]])
            T.op(DVE, lambda: nc.vector.reciprocal(out=stat[:, c + 1:c + 2], in_=stat[:, c + 1:c + 2]), r=[stB[s]], w=[stB[s]])
        for s in range(n):
            c = 2 * s
            p = s % nxn
            T.op(ACT, lambda: nc.scalar.activation(out=xn_v[p], in_=xin[s], func=AF.Copy,
                                                   scale=stat[:, c + 1:c + 2]), r=xinB[s] + [stB[s]], w=[xnB[p]])
            pt = PST2[s % 2]
            fns = [(lambda k=k: nc.tensor.transpose(pt[:, k, :], xn_v[p][:, k * 128:(k + 1) * 128], cm(C_IDENT)))
                   for k in range(KC)]
            T.group(PE, fns, r=[xnB[p], cstB], w=PSTB[s % 2])
            T.op(DVE, lambda: nc.vector.tensor_tensor(
                out=dstT[:, :, s * 128:(s + 1) * 128], in0=pt,
                in1=cv[:, gcol:gcol + KC].unsqueeze(2).to_broadcast([128, KC, 128]), op=ALU.mult),
                r=PSTB[s % 2] + [cvB], w=[dstB])

    hTB = buf("hT")
    kTB, vCB = buf("kT"), buf("vC")
    kmTB, vmB = buf("kmT"), buf("vm")
    ybB, ycB = buf("ybT"), buf("ycT")
    yaBc = [buf(f"ya{h}") for h in range(16)]
    x1B = [buf(f"x1_{s}") for s in range(NSUB)]
    qTB = buf("qT")
    hstB, haloB = buf("hstate"), buf("halo")
    R2B = buf("R2scratch")

    r2_live = []

    def r2buf(name):
        b = buf(name)
        if b not in r2_live:
            r2_live.append(b)
        return b

    mgB = r2buf("mergedT")

    def r2_handoff(newnames):
        news = [r2buf(n) for n in newnames]
        olds = [b for b in r2_live if b not in news]
        T.handoff(olds, news)

    def mem_prep(b):
        r2_handoff(["xs0", "xs1", "xs2", "xs3", "xn0", "xn1", "xn2", "xn3"])
        rows = [mem_d[b * MEM + s * 128: b * MEM + (s + 1) * 128, :] for s in range(2)]
        norm_transpose(rows, 2, CV_GMEM, hT, hTB, None, 0)
        mT = hT
        sq_v = [r2(24576 - 4096 + i * 512, [128, MEM], BF16) for i in range(2)]
        sm_v = r2(24576 - 4096 + 1024, [128, MEM], F32)
        sqB = [r2buf("msq0"), r2buf("msq1")]
        smB = r2buf("msm")
        T.handoff([buf("xn1")], sqB + [smB])
        for cbk in range(2):
            Wap, Wb = load_w("kv", cbk)
            for hp in range(2):
                pbs = [0 + 2 * hp, 1 + 2 * hp]
                for e in range(2):
                    dense_ws(Wap, Wb, (2 * hp + e) * 128, mT, [hTB], KC, pbs[e], ncols=MEM)
                    T.op(ACT, lambda e=e: nc.scalar.activation(out=sq_v[e], in_=bank(pbs[e])[:, 0:MEM], func=AF.Square),
                         r=[bankB[pbs[e]]], w=[sqB[e]])
                T.group(PE, [lambda: nc.tensor.matmul(bank(4)[:, 0:MEM], lhsT=cm(C_O256), rhs=sq_v[0], start=True, stop=False),
                             lambda: nc.tensor.matmul(bank(4)[:, 0:MEM], lhsT=cm(C_O256), rhs=sq_v[1], start=False, stop=True)],
                        r=sqB + [cstB], w=[bankB[4]])
                T.op(ACT, lambda: nc.scalar.activation(out=sm_v, in_=bank(4)[:, 0:MEM], func=AF.Sqrt, bias=EPS, scale=1.0),
                     r=[bankB[4]], w=[smB])
                T.op(DVE, lambda: nc.vector.reciprocal(out=sm_v, in_=sm_v), r=[smB], w=[smB])
                for e in range(2):
                    ch = cbk * 4 + hp * 2 + e
                    T.op(DVE, lambda e=e, ch=ch: nc.vector.scalar_tensor_tensor(
                        out=kmT[:, ch, :], in0=bank(pbs[e])[:, 0:MEM], scalar=cv[:, CV_MGK + e:CV_MGK + e + 1],
                        in1=sm_v, op0=ALU.mult, op1=ALU.mult), r=[bankB[pbs[e]], smB, cvB], w=[kmTB])
        for cbv in range(2):
            Wap, Wb = load_w("kv", 2 + cbv)
            for mb in range(2):
                pb = 5 if mb == 0 else 3
                dense_as(Wap, Wb, mT, [hTB], KC, mb, pb)
                T.op(ACT, lambda mb=mb, pb=pb: nc.scalar.activation(out=vm[:, mb, cbv * 512:(cbv + 1) * 512], in_=bank(pb),
                                                                    func=AF.Copy), r=[bankB[pb]], w=[vmB])
        dump("kmT", kmT[:], [128, 8, MEM], BF16, [kmTB])
        dump("vm", vm[:], [128, 2, 1024], BF16, [vmB])

    def tile(b, qt):
        tok0 = b * SEQ + qt * TT
        first = (qt == 0)
        last_dbg = (b == NSEQ - 1 and qt == NT - 1)

        r2_handoff(["xs0", "xs1", "xs2", "xs3", "xn0", "xn1", "xn2", "xn3"])
        rows = [x_d[tok0 + s * 128: tok0 + (s + 1) * 128, :] for s in range(NSUB)]
        norm_transpose(rows, NSUB, CV_GMIX, hT, hTB, None, 0)
        if last_dbg:
            dump("hT", hT[:], [128, KC, TT], BF16, [hTB])

        def mr(off, shape, dt):
            return view(MR2, off, shape, dt)

        L_UB = 8192
        L_C = L_UB + 2 * 2080
        L_CBF = L_C + 3 * 2048
        L_R = L_CBF + 2048
        L_A = L_R + 2048
        L_IG = L_A + 2048
        assert L_IG + 2048 <= 40960
        ub_v = [mr(L_UB + i * 2080, [128, 520], F32) for i in range(2)]
        c_v = [mr(L_C + i * 2048, [128, TT], F32) for i in range(3)]
        cbf_v = [mr(L_CBF + i * 1024, [128, TT], BF16) for i in range(2)]
        r_v = mr(L_R, [128, TT], F32)
        a_v = mr(L_A, [128, TT], F32)
        ig_v = mr(L_IG, [128, TT], F32)
        lru_names = ["ub0", "ub1", "c0", "c1", "c2", "cbf0", "cbf1", "lr", "la", "lig"]
        r2_handoff(lru_names + ["qT", "sq0", "sq1", "sm0", "sm1"])
        ubB = [r2buf("ub0"), r2buf("ub1")]
        cB = [r2buf("c0"), r2buf("c1"), r2buf("c2")]
        cbfB = [r2buf("cbf0"), r2buf("cbf1")]
        rB, aB, igB = r2buf("lr"), r2buf("la"), r2buf("lig")
        if first:
            T.op(DVE, lambda: nc.vector.memset(hstate[:], 0.0), w=[hstB])
            T.op(DVE, lambda: nc.vector.memset(halo[:], 0.0), w=[haloB])
        dense_pool = [0, 1, 2, 3]
        gate_pool = [4, 5]
        lw = {}

        def S1(h):
            blk, c = divmod(h, 4)
            if c == 0:
                Wg, Wgb = load_w("in", CB_LRUG + blk)
                for cc in range(4):
                    pb = next_bank(dense_pool)
                    dense_ws(Wg, Wgb, cc * 128, hT, [hTB], KC, pb)
                    hh = blk * 4 + cc
                    T.op(ACT, lambda: nc.scalar.activation(out=yaT[:, hh, :], in_=bank(pb), func=AF.Gelu),
                         r=[bankB[pb]], w=[yaBc[hh]])
                lw["x"] = load_w("in", CB_LRUX + blk)
            Wx, Wxb = lw["x"]
            pb = next_bank(dense_pool)
            dense_ws(Wx, Wxb, c * 128, hT, [hTB], KC, pb)
            lw[("pb", h)] = pb

        def S1b(h):
            p = h % 2
            pb = lw[("pb", h)]
            T.op(DVE, lambda: nc.vector.tensor_copy(out=ub_v[p][:, 3:515], in_=bank(pb)), r=[bankB[pb]], w=[ubB[p]])

        def S2(h):
            p, q = h % 2, h % 3
            T.op(DVE, lambda: nc.vector.tensor_copy(out=ub_v[p][:, 0:3], in_=halo[:, h, :]), r=[haloB], w=[ubB[p]])
            T.op(DVE, lambda: nc.vector.tensor_scalar(
                out=c_v[q], in0=ub_v[p][:, 0:512], scalar1=cv[:, CV_CONVW + h:CV_CONVW + h + 1],
                scalar2=cv[:, CV_CONVB + h:CV_CONVB + h + 1], op0=ALU.mult, op1=ALU.add), r=[ubB[p], cvB], w=[cB[q]])
            for k in range(1, 4):
                T.op(DVE, lambda: nc.vector.scalar_tensor_tensor(
                    out=c_v[q], in0=ub_v[p][:, k:k + 512], scalar=cv[:, CV_CONVW + 16 * k + h:CV_CONVW + 16 * k + h + 1],
                    in1=c_v[q], op0=ALU.mult, op1=ALU.add), r=[ubB[p], cvB, cB[q]], w=[cB[q]])
            T.op(DVE, lambda: nc.vector.tensor_copy(out=halo[:, h, :], in_=ub_v[p][:, 512:515]), r=[ubB[p]], w=[haloB])

        def S3a(h):
            p, q = h % 2, h % 3
            T.op(DVE, lambda: nc.vector.tensor_copy(out=cbf_v[p], in_=c_v[q]), r=[cB[q]], w=[cbfB[p]])

        def S3b(h):
            p = h % 2
            T.group(PE, [lambda: nc.tensor.matmul(bank(4), lhsT=lruw[:, 0, h, :], rhs=cbf_v[p], start=True, stop=True)],
                    r=[lruwB, cbfB[p]], w=[bankB[4]])
            T.group(PE, [lambda: nc.tensor.matmul(bank(5), lhsT=lruw[:, 1, h, :], rhs=cbf_v[p], start=True, stop=True)],
                    r=[lruwB, cbfB[p]], w=[bankB[5]])

        def S4(h):
            q = h % 3
            T.op(ACT, lambda: nc.scalar.activation(out=r_v, in_=bank(4), func=AF.Tanh,
                                                   bias=dv[:, DV_HBA + h:DV_HBA + h + 1], scale=0.5), r=[bankB[4], dvB], w=[rB])
            T.op(ACT, lambda: nc.scalar.activation(out=ig_v, in_=bank(5), func=AF.Tanh,
                                                   bias=dv[:, DV_HBX + h:DV_HBX + h + 1], scale=0.5), r=[bankB[5], dvB], w=[igB])
            T.op(ACT, lambda: nc.scalar.activation(out=a_v, in_=r_v, func=AF.Exp, scale=dv[:, DV_NSPH + h:DV_NSPH + h + 1],
                                                   bias=dv[:, DV_NSPH + h:DV_NSPH + h + 1]), r=[rB, dvB], w=[aB])
            T.op(ACT, lambda: nc.scalar.activation(out=r_v, in_=r_v, func=AF.Exp, scale=dv[:, DV_NSP + h:DV_NSP + h + 1],
                                                   bias=dv[:, DV_NSP + h:DV_NSP + h + 1]), r=[rB, dvB], w=[rB])
            T.op(ACT, lambda: nc.scalar.activation(out=r_v, in_=r_v, func=AF.Sqrt, bias=1.0, scale=-1.0), r=[rB], w=[rB])
            if first:
                T.op(DVE, lambda: nc.vector.memset(r_v[:, 0:1], 1.0), r=[rB], w=[rB])
            T.op(DVE, lambda: nc.vector.scalar_tensor_tensor(out=ig_v, in0=ig_v, scalar=1.0, in1=c_v[q],
                                                             op0=ALU.add, op1=ALU.mult), r=[igB, cB[q]], w=[igB])
            T.op(DVE, lambda: nc.vector.scalar_tensor_tensor(out=ig_v, in0=ig_v, scalar=0.5, in1=r_v,
                                                             op0=ALU.mult, op1=ALU.mult), r=[igB, rB], w=[igB])
            T.op(DVE, lambda: nc.vector.tensor_tensor_scan(
                out=c_v[q], data0=a_v, data1=ig_v, initial=hstate[:, h:h + 1], op0=ALU.mult, op1=ALU.add),
                r=[aB, igB, hstB, cB[q]], w=[cB[q]])
            T.op(DVE, lambda: nc.vector.tensor_copy(out=hstate[:, h:h + 1], in_=c_v[q][:, 511:512]), r=[cB[q]], w=[hstB])
            T.op(DVE, lambda: nc.vector.tensor_tensor(out=yaT[:, h, :], in0=c_v[q], in1=yaT[:, h, :], op=ALU.mult),
                 r=[cB[q], yaBc[h]], w=[yaBc[h]])

        A_Q = 0
        A_E = 8192
        A_SP = A_E + 12288
        A_T = A_SP + 4096
        A_W = A_T + 8192
        A_ACC = A_W + 4096
        assert A_ACC + 2048 <= 40960
        A_SQ = A_W
        A_SM = A_W + 2048
        qT = mr(A_Q, [128, 8, TT], BF16)
        sq_v = [mr(A_SQ + i * 1024, [128, TT], BF16) for i in range(2)]
        sm_v = [mr(A_SM + i * 2048, [128, TT], F32) for i in range(2)]
        e_v = [mr(A_E + i * 4096, [128, 2, TT], F32) for i in range(3)]
        sp_v = [mr(A_SP + i * 2048, [128, 2, TT], BF16) for i in range(2)]
        t_v = [mr(A_T + i * 4096, [128, 2, TT], F32) for i in range(2)]
        w_v = [mr(A_W + i * 2048, [128, 2, TT], BF16) for i in range(2)]
        acc_v = mr(A_ACC, [128, 2, TT], BF16)
        qTB_ = r2buf("qT")
        sqB = [r2buf("sq0"), r2buf("sq1")]
        smB = [r2buf("sm0"), r2buf("sm1")]
        eB = [r2buf(f"ae{i}") for i in range(3)]
        spB = [r2buf(f"asp{i}") for i in range(2)]
        tB = [r2buf(f"at{i}") for i in range(2)]
        wB = [r2buf(f"aw{i}") for i in range(2)]
        accB = r2buf("acc")
        nrm = {"i": 0}

        def qk_norm(Wap, Wb, c, gcol_ap, out_ap, outB):
            i = nrm["i"] % 2
            nrm["i"] += 1
            pb = next_bank(dense_pool)
            dense_ws(Wap, Wb, c * 128, hT, [hTB], KC, pb)
            T.op(ACT, lambda: nc.scalar.activation(out=sq_v[i], in_=bank(pb), func=AF.Square), r=[bankB[pb]], w=[sqB[i]])
            pm = next_bank(gate_pool)
            T.group(PE, [lambda: nc.tensor.matmul(bank(pm), lhsT=cm(C_B64), rhs=sq_v[i], start=True, stop=True)],
                    r=[sqB[i], cstB], w=[bankB[pm]])
            T.op(ACT, lambda: nc.scalar.activation(out=sm_v[i], in_=bank(pm), func=AF.Ln, bias=EPS, scale=1.0),
                 r=[bankB[pm]], w=[smB[i]])
            T.op(ACT, lambda: nc.scalar.activation(out=sm_v[i], in_=sm_v[i], func=AF.Exp, scale=-0.5), r=[smB[i]], w=[smB[i]])
            T.op(DVE, lambda: nc.vector.scalar_tensor_tensor(out=out_ap, in0=bank(pb), scalar=gcol_ap, in1=sm_v[i],
                                                             op0=ALU.mult, op1=ALU.mult),
                 r=[bankB[pb], smB[i], cvB, dvB], w=[outB])

        kw = {}

        def ktask(i, c):
            def run():
                if c == 0:
                    kw["w"] = load_w("in", CB_K + i)
                Wk, Wkb = kw["w"]
                qk_norm(Wk, Wkb, c, cv[:, CV_GK:CV_GK + 1], kT[:, i * 4 + c, qt * TT:(qt + 1) * TT], kTB)
            return run

        ktasks = [ktask(i, c) for i in range(2) for c in range(4)]
        for t in range(16 + 3):
            if 0 <= t - 2 < 16:
                S3a(t - 2)
            if 0 <= t - 1 < 16:
                S2(t - 1)
            if 0 <= t - 3 < 16:
                S4(t - 3)
            if t < 16:
                S1(t)
            if t >= 15:
                for _ in range(2):
                    if ktasks:
                        ktasks.pop(0)()
            if 0 <= t - 2 < 16:
                S3b(t - 2)
            if t < 16:
                S1b(t)
        if last_dbg:
            dump("yaT", yaT, [128, 16, TT], BF16, yaBc)

        while ktasks:
            ktasks.pop(0)()
        T.handoff([r2buf(nm) for nm in lru_names],
                  [r2buf(f"ae{i}") for i in range(3)] + [r2buf(f"asp{i}") for i in range(2)] + [r2buf(f"at{i}") for i in range(2)])
        for i in range(2):
            Wv, Wvb = load_w("in", CB_V + i)
            for s in range(NSUB):
                pb = next_bank(dense_pool)
                dense_as(Wv, Wvb, hT, [hTB], KC, s, pb)
                T.op(ACT, lambda: nc.scalar.activation(out=vC[:, qt * 4 + s, i * 512:(i + 1) * 512],
                                                       in_=bank(pb), func=AF.Copy), r=[bankB[pb]], w=[vCB])
        for i in range(2):
            Wq, Wqb = load_w("in", CB_Q + i)
            for c in range(4):
                qk_norm(Wq, Wqb, c, dv[:, DV_GQ8:DV_GQ8 + 1], qT[:, i * 4 + c, :], qTB_)
        if last_dbg:
            dump("qT", qT, [128, 8, TT], BF16, [qTB_])
            dump("kT", kT[:], [128, 8, SEQ], BF16, [kTB])
            dump("vC", vC[:], [128, SEQ // 128, 1024], BF16, [vCB])

        nkb = 4 * qt + 4
        ZB = [0, 1]
        AB = [2, 3]
        o_pool = [4, 5]

        def geom(n):
            kb = nkb - 1 - n
            di = kb - 4 * qt
            c0 = max(di, 0) * 128
            return kb, di, c0, TT - c0

        T.handoff(sqB + smB, wB + [accB])
        mask2 = cm(C_MASK).unsqueeze(1).to_broadcast([128, 2, 128])

        steps = [(c, n) for c in range(8) for n in range(nkb)]
        NS = len(steps)
        po_of = [next_bank(o_pool) for c in range(8)]

        def pe_z(g):
            c, n = steps[g]
            kb, di, c0, nco = geom(n)
            for e in range(2):
                pr = slice(e * 64, (e + 1) * 64)
                T.group(PE, [lambda: nc.tensor.matmul(bank(e)[:, 0:nco], lhsT=kT[pr, c, kb * 128:(kb + 1) * 128],
                                                      rhs=qT[pr, c, c0:TT], start=True, stop=True)],
                        r=[kTB, qTB_], w=[bankB[e]])

        def pe_cum(g):
            c, n = steps[g]
            kb, di, c0, nco = geom(n)
            i2 = g % 2
            for e in range(2):
                pa = 2 + e
                fns = [lambda: nc.tensor.matmul(bank(pa)[:, 0:nco], lhsT=cm(C_NUINCL), rhs=sp_v[i2][:, e, 0:nco],
                                                start=True, stop=(n == 0))]
                if n > 0:
                    fns.append(lambda: nc.tensor.matmul(bank(pa)[:, 0:nco], lhsT=cm(C_NEGONES), rhs=acc_v[:, e, c0:TT],
                                                        start=False, stop=True))
                T.group(PE, fns, r=[spB[i2], accB, cstB], w=[bankB[pa]])

        def pe_pv(g):
            c, n = steps[g]
            kb, di, c0, nco = geom(n)
            i2 = g % 2
            po = po_of[c]
            for e in range(2):
                pr = slice(e * 64, (e + 1) * 64)
                hd = 2 * c + e
                T.group(PE, [lambda: nc.tensor.matmul(bank(po)[pr, c0:TT], lhsT=vC[:, kb, hd * 64:(hd + 1) * 64],
                                                      rhs=w_v[i2][:, e, 0:nco], start=(n == 0), stop=(n == nkb - 1),
                                                      skip_group_check=True)],
                        r=[vCB, wB[i2]], w=[bankB[po]])
            if n == nkb - 1:
                T.op(DVE, lambda: nc.vector.tensor_copy(out=ybT[:, c, :], in_=bank(po)), r=[bankB[po]], w=[ybB])

        def act_z(g):
            c, n = steps[g]
            kb, di, c0, nco = geom(n)
            i3, i2 = g % 3, g % 2
            T.op(ACT, lambda: nc.scalar.activation(out=e_v[i3][:, :, 0:nco], in_=PS[:, 0:2, 0:nco], func=AF.Exp),
                 r=[bankB[0], bankB[1]], w=[eB[i3]])
            T.op(ACT, lambda: nc.scalar.activation(out=sp_v[i2][:, :, 0:nco], in_=e_v[i3][:, :, 0:nco], func=AF.Ln,
                                                   bias=1.0, scale=1.0), r=[eB[i3]], w=[spB[i2]])

        def act_t(g):
            c, n = steps[g]
            kb, di, c0, nco = geom(n)
            i2 = g % 2
            T.op(ACT, lambda: nc.scalar.activation(out=t_v[i2][:, :, 0:nco], in_=PS[:, 2:4, 0:nco], func=AF.Exp),
                 r=[bankB[2], bankB[3]], w=[tB[i2]])

        def dve_w(g):
            c, n = steps[g]
            kb, di, c0, nco = geom(n)
            i3, i2 = g % 3, g % 2
            T.op(DVE, lambda: nc.vector.tensor_tensor(out=w_v[i2][:, :, 0:nco], in0=e_v[i3][:, :, 0:nco],
                                                      in1=t_v[i2][:, :, 0:nco], op=ALU.mult),
                 r=[eB[i3], tB[i2]], w=[wB[i2]])
            if di >= 0:
                T.op(DVE, lambda: nc.vector.tensor_tensor(out=w_v[i2][:, :, 0:128], in0=w_v[i2][:, :, 0:128],
                                                          in1=mask2, op=ALU.mult), r=[wB[i2], cstB], w=[wB[i2]])

        def dve_acc(g):
            c, n = steps[g]
            kb, di, c0, nco = geom(n)
            i2 = g % 2
            if n == 0:
                T.op(DVE, lambda: nc.vector.memset(acc_v, 0.0), w=[accB])
            if n < nkb - 1:
                T.op(DVE, lambda: nc.vector.tensor_tensor(out=acc_v[:, :, c0:TT], in0=acc_v[:, :, c0:TT],
                                                          in1=sp_v[i2][:, :, 0:nco], op=ALU.add),
                     r=[accB, spB[i2]], w=[accB])

        def dve_mask_sp(g):
            c, n = steps[g]
            kb, di, c0, nco = geom(n)
            i2 = g % 2
            if di >= 0:
                T.op(DVE, lambda: nc.vector.tensor_tensor(out=sp_v[i2][:, :, 0:128], in0=sp_v[i2][:, :, 0:128],
                                                          in1=mask2, op=ALU.mult), r=[spB[i2], cstB], w=[spB[i2]])

        for g in range(NS + 3):
            if g < NS:
                pe_z(g)
            if 0 <= g - 3 < NS:
                pe_pv(g - 3)
            if 0 <= g - 1 < NS:
                pe_cum(g - 1)
            if g < NS:
                act_z(g)
            if 0 <= g - 1 < NS:
                act_t(g - 1)
            if 0 <= g - 2 < NS:
                dve_w(g - 2)
            if 0 <= g - 1 < NS:
                dve_acc(g - 1)
            if g < NS:
                dve_mask_sp(g)
        if last_dbg:
            dump("ybT", ybT, [128, 8, TT], BF16, [ybB])


        D_Q = 0
        D_SQ = 8192
        D_SM = D_SQ + 2048
        D_EM = D_SM + 2048
        D_RD = D_EM + 4096
        assert D_RD + 4096 <= 24576
        qmT = r2(D_Q, [128, 8, TT], BF16)
        msq_v = [r2(D_SQ + i * 1024, [128, TT], BF16) for i in range(2)]
        msm_v = r2(D_SM, [128, TT], F32)
        em_v = [r2(D_EM + i * 1024, [128, TT], BF16) for i in range(4)]
        rd_v = [r2(D_RD + i * 2048, [128, TT], F32) for i in range(2)]
        names = ["qmT", "dsq0", "dsq1", "dsm", "em0", "em1", "em2", "em3", "rd0", "rd1"]
        r2_handoff(names)
        qmB = r2buf("qmT")
        dsqB = [r2buf("dsq0"), r2buf("dsq1")]
        dsmB = r2buf("dsm")
        emB = [r2buf(f"em{i}") for i in range(4)]
        rdB = [r2buf("rd0"), r2buf("rd1")]
        for i in range(2):
            Wq, Wqb = load_w("in", CB_QM + i)
            for hp in range(2):
                hm = 2 * i + hp
                pbs = [2 * hp, 2 * hp + 1]
                for e in range(2):
                    dense_ws(Wq, Wqb, (2 * hp + e) * 128, hT, [hTB], KC, pbs[e])
                    T.op(ACT, lambda e=e: nc.scalar.activation(out=msq_v[e], in_=bank(pbs[e]), func=AF.Square),
                         r=[bankB[pbs[e]]], w=[dsqB[e]])
                pm = next_bank(gate_pool)
                T.group(PE, [lambda: nc.tensor.matmul(bank(pm), lhsT=cm(C_O256), rhs=msq_v[0], start=True, stop=False),
                             lambda: nc.tensor.matmul(bank(pm), lhsT=cm(C_O256), rhs=msq_v[1], start=False, stop=True)],
                        r=dsqB + [cstB], w=[bankB[pm]])
                T.op(ACT, lambda: nc.scalar.activation(out=msm_v, in_=bank(pm), func=AF.Ln, bias=EPS, scale=1.0),
                     r=[bankB[pm]], w=[dsmB])
                T.op(ACT, lambda: nc.scalar.activation(out=msm_v, in_=msm_v, func=AF.Exp, scale=-0.5), r=[dsmB], w=[dsmB])
                for e in range(2):
                    T.op(DVE, lambda e=e: nc.vector.scalar_tensor_tensor(
                        out=qmT[:, 2 * hm + e, :], in0=bank(pbs[e]), scalar=dv[:, DV_MGQ + e:DV_MGQ + e + 1],
                        in1=msm_v, op0=ALU.mult, op1=ALU.mult), r=[bankB[pbs[e]], dsmB, dvB], w=[qmB])
        if last_dbg:
            dump("qmT", qmT, [128, 8, TT], BF16, [qmB])
        z_pool = [0, 1]
        a_pool = [2, 3]
        for hm in range(4):
            ei = (hm % 2) * 2
            for mb in range(2):
                pz = next_bank(z_pool)
                T.group(PE, [lambda e=e: nc.tensor.matmul(bank(pz), lhsT=kmT[:, 2 * hm + e, mb * 128:(mb + 1) * 128],
                                                          rhs=qmT[:, 2 * hm + e, :], start=(e == 0), stop=(e == 1))
                             for e in range(2)], r=[kmTB, qmB], w=[bankB[pz]])
                T.op(ACT, lambda mb=mb: nc.scalar.activation(out=em_v[ei + mb], in_=bank(pz), func=AF.Exp),
                     r=[bankB[pz]], w=[emB[ei + mb]])
            pd = next_bank(a_pool)
            T.group(PE, [lambda mb=mb: nc.tensor.matmul(bank(pd), lhsT=cm(C_ONES), rhs=em_v[ei + mb], start=(mb == 0), stop=(mb == 1))
                         for mb in range(2)], r=[emB[ei], emB[ei + 1], cstB], w=[bankB[pd]])
            ri = hm % 2
            T.op(DVE, lambda: nc.vector.reciprocal(out=rd_v[ri], in_=bank(pd)), r=[bankB[pd]], w=[rdB[ri]])
            for e in range(2):
                po = next_bank(o_pool)
                T.group(PE, [lambda mb=mb: nc.tensor.matmul(bank(po), lhsT=vm[:, mb, (2 * hm + e) * 128:(2 * hm + e + 1) * 128],
                                                            rhs=em_v[ei + mb], start=(mb == 0), stop=(mb == 1))
                             for mb in range(2)], r=[vmB, emB[ei], emB[ei + 1]], w=[bankB[po]])
                T.op(DVE, lambda: nc.vector.tensor_tensor(out=ycT[:, 2 * hm + e, :], in0=bank(po), in1=rd_v[ri], op=ALU.mult),
                     r=[bankB[po], rdB[ri]], w=[ycB])
        if last_dbg:
            dump("ycT", ycT, [128, 8, TT], BF16, [ycB])

        E_SIG = 0
        E_ACC = 8192
        E_TMP = 16384
        sig_v = [r2(E_SIG + i * 2048, [128, TT], F32) for i in range(4)]
        macc_v = [r2(E_ACC + i * 2048, [128, TT], F32) for i in range(4)]
        tmp_v = [r2(E_TMP + i * 2048, [128, TT], F32) for i in range(2)]
        names = [f"sig{i}" for i in range(4)] + [f"macc{i}" for i in range(4)] + ["mtmp0", "mtmp1", "mergedT"]
        r2_handoff(names)
        sigB = [r2buf(f"sig{i}") for i in range(4)]
        maccB = [r2buf(f"macc{i}") for i in range(4)]
        tmpB = [r2buf("mtmp0"), r2buf("mtmp1")]
        allp = [0, 1, 2, 3, 4, 5]
        ti = 0
        for cbo in range(4):
            for br in range(3):
                Wg, Wgb = load_w("in", CB_GATE + br * 4 + cbo)
                for c in range(4):
                    pb = next_bank(allp)
                    dense_ws(Wg, Wgb, c * 128, hT, [hTB], KC, pb)
                    col = CV_BGATE + br * 16 + cbo * 4 + c
                    T.op(ACT, lambda pb=pb, c=c, col=col: nc.scalar.activation(out=sig_v[c], in_=bank(pb), func=AF.Sigmoid,
                                                                               bias=cv[:, col:col + 1], scale=1.0),
                         r=[bankB[pb], cvB], w=[sigB[c]])
                if br == 0:
                    Wp, Wpb = load_w("pa", cbo)
                    src, srcB, nk = yaT, yaBc, 16
                elif br == 1:
                    Wp, Wpb = load_w("pb", cbo, 0, 8)
                    src, srcB, nk = ybT, [ybB], 8
                else:
                    Wp, Wpb = load_w("pc", cbo, 0, 8)
                    src, srcB, nk = ycT, [ycB], 8
                for c in range(4):
                    pb = next_bank(allp)
                    dense_ws(Wp, Wpb, c * 128, src, srcB, nk, pb)
                    if br == 0:
                        T.op(DVE, lambda pb=pb, c=c: nc.vector.tensor_tensor(out=macc_v[c], in0=bank(pb), in1=sig_v[c], op=ALU.mult),
                             r=[bankB[pb], sigB[c]], w=[maccB[c]])
                    else:
                        t = ti % 2
                        ti += 1
                        T.op(DVE, lambda pb=pb, c=c, t=t: nc.vector.tensor_tensor(out=tmp_v[t], in0=bank(pb), in1=sig_v[c], op=ALU.mult),
                             r=[bankB[pb], sigB[c]], w=[tmpB[t]])
                        if br == 1:
                            T.op(DVE, lambda c=c, t=t: nc.vector.tensor_tensor(out=macc_v[c], in0=macc_v[c], in1=tmp_v[t], op=ALU.add),
                                 r=[maccB[c], tmpB[t]], w=[maccB[c]])
                        else:
                            T.op(DVE, lambda c=c, t=t: nc.vector.tensor_tensor(out=mergedT[:, cbo * 4 + c, :], in0=macc_v[c],
                                                                                in1=tmp_v[t], op=ALU.add),
                                 r=[maccB[c], tmpB[t]], w=[mgB])
        if last_dbg:
            dump("mergedT", mergedT, [128, 16, TT], BF16, [mgB])

        T.handoff(yaBc[0:8], [x1B[0]])
        T.handoff(yaBc[8:16], [x1B[1]])
        T.handoff([ybB], [x1B[2]])
        T.handoff([ycB], [x1B[3]])
        for s in range(NSUB):
            T.dma(SP, x1[:, s, :], x_d[tok0 + s * 128: tok0 + (s + 1) * 128, :], x1sem[s], w=[x1B[s]])
        for cb in range(4):
            Wo, Wob = load_w("out", cb)
            for s in range(NSUB):
                pb = next_bank(allp)
                dense_as(Wo, Wob, mergedT, [mgB], KC, s, pb)
                T.op(DVE, lambda pb=pb, s=s, cb=cb: nc.vector.tensor_tensor(
                    out=x1[:, s, cb * 512:(cb + 1) * 512], in0=bank(pb), in1=x1[:, s, cb * 512:(cb + 1) * 512], op=ALU.add),
                    r=[bankB[pb], x1B[s]], w=[x1B[s]])
        if last_dbg:
            dump("x1", x1, [128, NSUB, D], F32, x1B)

        r2_handoff(["xs0", "xs1", "xs2", "xs3", "xn0", "xn1", "xn2", "xn3"])
        norm_transpose([x1[:, s, :] for s in range(NSUB)], NSUB, CV_GFFN, hT, hTB, None, 0, load=False,
                       srcs=[[x1B[s]] for s in range(NSUB)])

        H_ACT = 0
        H_SG = 24576
        actg = [view(MR2, H_ACT + i * 12288, [128, 12, TT], BF16) for i in range(2)]
        sg_v = [view(MR2, H_SG + i * 2048, [128, TT], F32) for i in range(8)]
        names = ["actg0", "actg1"] + [f"sg{i}" for i in range(8)]
        r2_handoff(names)
        actB = [r2buf("actg0"), r2buf("actg1")]
        sgB = [r2buf(f"sg{i}") for i in range(8)]
        T.handoff([mgB], actB + sgB)
        groups = [[0, 1, 2], [3, 4, 5], [6, 7, 8], [9, 10]]
        sgi = 0
        for gi_, blks in enumerate(groups):
            ag = actg[gi_ % 2]
            agB = actB[gi_ % 2]
            for bi_, blk in enumerate(blks):
                Wg, Wgb = load_w("fc", blk)
                sidx = []
                for c in range(4):
                    pb = next_bank(allp)
                    dense_ws(Wg, Wgb, c * 128, hT, [hTB], KC, pb)
                    si = sgi % 8
                    sgi += 1
                    sidx.append(si)
                    T.op(ACT, lambda pb=pb, si=si: nc.scalar.activation(out=sg_v[si], in_=bank(pb), func=AF.Silu),
                         r=[bankB[pb]], w=[sgB[si]])
                Wu, Wub = load_w("fc", 11 + blk)
                for c in range(4):
                    pb = next_bank(allp)
                    dense_ws(Wu, Wub, c * 128, hT, [hTB], KC, pb)
                    si = sidx[c]
                    T.op(DVE, lambda pb=pb, si=si, bi_=bi_, c=c: nc.vector.tensor_tensor(
                        out=ag[:, bi_ * 4 + c, :], in0=bank(pb), in1=sg_v[si], op=ALU.mult),
                        r=[bankB[pb], sgB[si]], w=[agB])
            nk = 4 * len(blks)
            k0 = 4 * blks[0]
            for cb in range(4):
                Wd, Wdb = load_w("down", cb, k0, nk)
                for s in range(NSUB):
                    pb = next_bank(allp)
                    dense_as(Wd, Wdb, ag, [agB], nk, s, pb)
                    T.op(DVE, lambda pb=pb, s=s, cb=cb: nc.vector.tensor_tensor(
                        out=x1[:, s, cb * 512:(cb + 1) * 512], in0=bank(pb), in1=x1[:, s, cb * 512:(cb + 1) * 512], op=ALU.add),
                        r=[bankB[pb], x1B[s]], w=[x1B[s]])
        for s in range(NSUB):
            T.dma(SP, y_d[tok0 + s * 128: tok0 + (s + 1) * 128, :], x1[:, s, :], osem[s], r=[x1B[s]])
        T.handoff(x1B, yaBc + [ybB, ycB])
        T.handoff(actB, [mgB])

    for b in range(NSEQ):
        mem_prep(b)
        for qt in range(NT):
            tile(b, qt)

    for s in range(NSUB):
        SP.h.wait_ge(osem[s][0], osem[s][1])
    if dbg_sem[1]:
        SP.h.wait_ge(dbg_sem[0], dbg_sem[1])
    stats = {e.name: (e.nops, e.nwaits) for e in (PE, ACT, DVE, POOL, SP)}
    return nc, dbg_outs, stats


def make_consts():
    c = np.zeros((128, NCONST, 128), np.float32)
    idx = np.arange(128)
    c[:, C_IDENT, :] = np.eye(128, dtype=np.float32)
    c[:, C_MASK, :] = (idx[:, None] < idx[None, :]).astype(np.float32)
    c[:, C_NUINCL, :] = -(idx[:, None] >= idx[None, :]).astype(np.float32)
    c[:, C_NEGONES, :] = -1.0
    blk = (idx[:, None] // 64 == idx[None, :] // 64).astype(np.float32) / 64.0
    c[:, C_B64, :] = blk
    c[:, C_O256, :] = 1.0 / 256.0
    c[:, C_ONES, :] = 1.0
    return c


def pack_cvec(g_mix, g_mem, b_gate, conv_w, conv_b, lru_ba, lru_bx, lru_lambda, sb_gq, sb_gk, mem_gq, mem_gk, g_ffn):
    cvec = np.zeros((128, CV_N), np.float32)
    cvec[:, CV_GMIX:CV_GMIX + 16] = g_mix.reshape(16, 128).T
    cvec[:, CV_GFFN:CV_GFFN + 16] = g_ffn.reshape(16, 128).T
    cvec[:, CV_GMEM:CV_GMEM + 16] = g_mem.reshape(16, 128).T
    for k in range(4):
        cvec[:, CV_CONVW + 16 * k:CV_CONVW + 16 * (k + 1)] = conv_w[k].reshape(16, 128).T
    cvec[:, CV_CONVB:CV_CONVB + 16] = conv_b.reshape(16, 128).T
    cvec[:, CV_BA:CV_BA + 16] = lru_ba.reshape(16, 128).T
    cvec[:, CV_BX:CV_BX + 16] = lru_bx.reshape(16, 128).T
    cvec[:, CV_LAM:CV_LAM + 16] = lru_lambda.reshape(16, 128).T
    cvec[:, CV_BGATE:CV_BGATE + 48] = b_gate.reshape(48, 128).T
    cvec[:, CV_GQ] = np.tile(sb_gq, 2)
    cvec[:, CV_GK] = np.tile(sb_gk, 2)
    cvec[:, CV_MGQ:CV_MGQ + 2] = mem_gq.reshape(2, 128).T
    cvec[:, CV_MGK:CV_MGK + 2] = mem_gk.reshape(2, 128).T
    return cvec


_NC_CACHE = {}


def kernel(x, mem, g_mix, g_mem, w_in, b_gate, conv_w, conv_b, lru_wa, lru_ba, lru_wx, lru_bx,
           lru_lambda, sb_gq, sb_gk, mem_w_kv, mem_gq, mem_gk, w_pa, w_pb, w_pc, w_out,
           g_ffn, w_fc, w_down):
    n = 8
    x = np.asarray(x, np.float32)
    mem = np.asarray(mem, np.float32)
    Bn, Tn, Dn = x.shape
    nseq = Bn // n
    f = lambda a: np.ascontiguousarray(np.asarray(a, np.float32))
    cvec = pack_cvec(f(g_mix)[0], f(g_mem)[0], f(b_gate)[0], f(conv_w)[0], f(conv_b)[0], f(lru_ba)[0], f(lru_bx)[0],
                     f(lru_lambda)[0], f(sb_gq)[0], f(sb_gk)[0], f(mem_gq)[0], f(mem_gk)[0], f(g_ffn)[0])
    consts = make_consts()
    lru_w = np.ascontiguousarray(np.stack([f(lru_wa)[0], f(lru_wx)[0]], axis=0))
    shared = {
        "w_in": f(w_in)[0], "w_kv": f(mem_w_kv)[0], "w_pa": f(w_pa)[0], "w_pb": f(w_pb)[0], "w_pc": f(w_pc)[0],
        "w_out": f(w_out)[0], "w_fc": f(w_fc)[0], "w_down": f(w_down)[0], "lru_w": lru_w, "cvec": cvec, "consts": consts,
    }
    key = (nseq, Tn)
    if key not in _NC_CACHE:
        _NC_CACHE[key] = build_nc(NSEQ=nseq, SEQ=Tn)[0]
    nc = _NC_CACHE[key]
    in_maps = []
    for c in range(n):
        m = dict(shared)
        m["x"] = np.ascontiguousarray(x[c * nseq:(c + 1) * nseq].reshape(nseq * Tn, Dn))
        m["mem"] = np.ascontiguousarray(mem[c * nseq:(c + 1) * nseq].reshape(nseq * MEM, Dn))
        in_maps.append(m)
    res = run_bass_kernel_spmd(nc, in_maps, core_ids=list(range(n)))
    out = np.concatenate([r["y"].reshape(nseq, Tn, Dn) for r in res.results], axis=0)
    return out.astype(np.float32)
```
